# Optimizing a Trainium2 kernel written in Bass

```python
import jax, jax.numpy as jnp
from jax import lax
import numpy as np

D_MODEL = 1024
BATCH = 8
SEQ = 2048
DEPTH = 2

D_MIX = D_MODEL
HEAD_DIM = 64
D_CONV = D_MIX // 4
CONV_GROUPS = D_CONV // HEAD_DIM
D_SGU = D_MIX // 4
SGU_HEADS = D_SGU // HEAD_DIM
D_SB = D_MIX - D_CONV - D_SGU
SB_HEADS = D_SB // HEAD_DIM
D_IN = 2 * D_CONV + 2 * D_SGU + 3 * D_SB
CONV_K = 31
CHUNK = 128
Q_BLOCK = 128
FFN_CONV_K = 3
D_FF = ((8 * D_MODEL // 3 + 127) // 128) * 128
EPS = 1e-6

kernel_name = "hybrid_conv_sgu_stickbreak_block"


def _rms(x):
    xf = x.astype(jnp.float32)
    return xf * lax.rsqrt(jnp.mean(xf * xf, axis=-1, keepdims=True) + EPS)


def rmsnorm(x, g):
    return (_rms(x) * g.astype(jnp.float32)).astype(x.dtype)


def layernorm(x, g, b):
    xf = x.astype(jnp.float32)
    mu = jnp.mean(xf, axis=-1, keepdims=True)
    var = jnp.mean(jnp.square(xf - mu), axis=-1, keepdims=True)
    y = (xf - mu) * lax.rsqrt(var + EPS)
    return (y * g.astype(jnp.float32) + b.astype(jnp.float32)).astype(x.dtype)


def causal_dwconv(x, w, b):
    c = x.shape[-1]
    k = w.shape[0]
    y = lax.conv_general_dilated(
        x, w[:, None, :].astype(x.dtype), window_strides=(1,), padding=[(k - 1, 0)],
        dimension_numbers=("NWC", "WIO", "NWC"), feature_group_count=c)
    return y + b.astype(x.dtype)


def conformer_conv(a_val, a_gate, conv_w, conv_b, ln_g, ln_b):
    h = a_val * jax.nn.sigmoid(a_gate)
    h = causal_dwconv(h, conv_w, conv_b)
    h = layernorm(h, ln_g, ln_b)
    return jax.nn.silu(h)


def chunked_sgu(u, v, ln_g, ln_b, w_s, b_s):
    bsz, s, c = u.shape
    u = jax.nn.gelu(u, approximate=False)
    v = layernorm(jax.nn.gelu(v, approximate=False), ln_g, ln_b)
    v = v.reshape(bsz, s // CHUNK, CHUNK, SGU_HEADS, HEAD_DIM)
    mask = jnp.tril(jnp.ones((CHUNK, CHUNK), dtype=bool))
    ws = jnp.where(mask[None], w_s, 0.0).astype(v.dtype)
    mixed = jnp.einsum("hts,bnshd->bnthd", ws, v)
    mixed = mixed + b_s.T.astype(v.dtype)[None, None, :, :, None]
    return u * mixed.reshape(bsz, s, c)


def stick_breaking_attention(q, k, v):
    bsz, s, h, d = q.shape
    scale = d ** -0.5
    q = q.transpose(0, 2, 1, 3)
    k = k.transpose(0, 2, 1, 3)
    v = v.transpose(0, 2, 1, 3)
    outs = []
    for i in range(s // Q_BLOCK):
        q0 = i * Q_BLOCK
        n = q0 + Q_BLOCK
        qb = q[:, :, q0:n]
        kb = k[:, :, :n]
        vb = v[:, :, :n]
        z = jnp.einsum("bhqd,bhkd->bhqk", qb, kb).astype(jnp.float32) * scale
        t_pos = q0 + jnp.arange(Q_BLOCK)[:, None]
        s_pos = jnp.arange(n)[None, :]
        mask = s_pos < t_pos
        log_1m_beta = jnp.where(mask, -jax.nn.softplus(z), 0.0)
        log_beta = -jax.nn.softplus(-z)
        log_a = log_beta + lax.cumsum(log_1m_beta, axis=3, reverse=True) - log_1m_beta
        a = jnp.where(mask, jnp.exp(log_a), 0.0)
        outs.append(jnp.einsum("bhqk,bhkd->bhqd", a.astype(vb.dtype), vb))
    o = jnp.concatenate(outs, axis=2).transpose(0, 2, 1, 3)
    return o.reshape(bsz, s, h * d)


def setup_inputs(seed: int = 0) -> dict:
    key = jax.random.key(seed)
    ks = jax.random.split(key, 20)
    f32 = jnp.float32

    def nrm(k, shape, scale):
        return jax.random.normal(k, shape, f32) * scale

    return {
        "x": nrm(ks[0], (BATCH, SEQ, D_MODEL), 1.0),
        "g_mix": 1.0 + nrm(ks[1], (DEPTH, D_MODEL), 0.02),
        "w_in": nrm(ks[2], (DEPTH, D_MODEL, D_IN), D_MODEL ** -0.5),
        "conv_w": nrm(ks[3], (DEPTH, CONV_K, D_CONV), CONV_K ** -0.5),
        "conv_b": nrm(ks[4], (DEPTH, D_CONV), 0.02),
        "conv_ln_g": 1.0 + nrm(ks[5], (DEPTH, D_CONV), 0.02),
        "conv_ln_b": nrm(ks[6], (DEPTH, D_CONV), 0.02),
        "sgu_ln_g": 1.0 + nrm(ks[7], (DEPTH, D_SGU), 0.02),
        "sgu_ln_b": nrm(ks[8], (DEPTH, D_SGU), 0.02),
        "sgu_w": nrm(ks[9], (DEPTH, SGU_HEADS, CHUNK, CHUNK), 0.5 * CHUNK ** -0.5),
        "sgu_b": 1.0 + nrm(ks[10], (DEPTH, SGU_HEADS, CHUNK), 0.02),
        "g_out": 1.0 + nrm(ks[11], (DEPTH, D_MIX), 0.02),
        "w_out": nrm(ks[12], (DEPTH, D_MIX, D_MODEL), D_MIX ** -0.5),
        "g_ffn": 1.0 + nrm(ks[13], (DEPTH, D_MODEL), 0.02),
        "w_up": nrm(ks[14], (DEPTH, D_MODEL, 2 * D_FF), D_MODEL ** -0.5),
        "ffn_conv_w": nrm(ks[15], (DEPTH, FFN_CONV_K, 2 * D_FF), FFN_CONV_K ** -0.5),
        "ffn_conv_b": nrm(ks[16], (DEPTH, 2 * D_FF), 0.02),
        "w_down": nrm(ks[17], (DEPTH, D_FF, D_MODEL), D_FF ** -0.5),
        "g_final": 1.0 + nrm(ks[18], (D_MODEL,), 0.02),
    }


def reference(x, g_mix, w_in, conv_w, conv_b, conv_ln_g, conv_ln_b, sgu_ln_g, sgu_ln_b,
              sgu_w, sgu_b, g_out, w_out, g_ffn, w_up, ffn_conv_w, ffn_conv_b, w_down,
              g_final):
    bsz, s, _ = x.shape
    splits = np.cumsum([D_CONV, D_CONV, D_SGU, D_SGU, D_SB, D_SB]).tolist()
    for l in range(DEPTH):
        h = rmsnorm(x, g_mix[l])
        p = h @ w_in[l]
        a_val, a_gate, b_u, b_v, c_q, c_k, c_v = jnp.split(p, splits, axis=-1)
        y_a = conformer_conv(a_val, a_gate, conv_w[l], conv_b[l], conv_ln_g[l], conv_ln_b[l])
        y_b = chunked_sgu(b_u, b_v, sgu_ln_g[l], sgu_ln_b[l], sgu_w[l], sgu_b[l])
        hs = (bsz, s, SB_HEADS, HEAD_DIM)
        y_c = stick_breaking_attention(c_q.reshape(hs), c_k.reshape(hs), c_v.reshape(hs))
        y = jnp.concatenate([_rms(y_a), _rms(y_b), _rms(y_c)], axis=-1)
        y = (y * g_out[l].astype(jnp.float32)).astype(x.dtype)
        x = x + y @ w_out[l]
        h = rmsnorm(x, g_ffn[l])
        up = causal_dwconv(h @ w_up[l], ffn_conv_w[l], ffn_conv_b[l])
        gate, val = jnp.split(up, 2, axis=-1)
        x = x + (jax.nn.silu(gate) * val) @ w_down[l]
    return rmsnorm(x, g_final)
```

```python
import os
import numpy as np
import ml_dtypes
import concourse.bass as bass
import concourse.mybir as mybir
from concourse.bass_utils import run_bass_kernel_spmd

F32 = mybir.dt.float32
BF16 = mybir.dt.bfloat16
AF = mybir.ActivationFunctionType
ALU = mybir.AluOpType

FULL_SYNC = not bool(os.environ.get("NO_FULL_SYNC"))
SPARSE_INC = not bool(os.environ.get("DENSE_INC"))
SGU_FIRST = not bool(os.environ.get("CONV_FIRST"))
ORDER = os.environ.get("ORDER", "akbcvq")
RANKED = int(os.environ.get("RANKED", "0"))
RANKED_F = int(os.environ.get("RANKED_F", "0"))
PENG = "dve" if os.environ.get("POOL_OFF") else "pool"
S = 2048
D = 1024
NT = 16
DIN = 2560
DFF = 2816
EPS = 1e-6
NEG = -30.0

PV_CW = 0
PV_CB = 62
PV_LG = 64
PV_LB = 66
PV_GA = 68
PV_SB = 70
PV_FW = 74
PV_FB = 206
NPV = 256
RW_GMIX = 0
RW_GFFN = 1024
RW_GOUT = 2048
RW_LNG = 2816
RW_LNB = 3072
NRW = 3328


class _FakeIns:
    def then_inc(self, *a, **k):
        return self


class _FakeEng:
    def __init__(self):
        self.call = None

    def __getattr__(self, name):
        def f(*a, **k):
            self.call = (name, a, k)
            return _FakeIns()
        return f


def _fsz(ap):
    n = 1
    for d in ap.shape[1:]:
        n *= int(d)
    return n


def _is_psum(ap):
    return str(ap.space) == "PSUM"


_TABS = {"Sigmoid": "sig", "Silu": "silu", "Gelu": "gelu", "Sqrt": "sqrt", "Ln": "lnexp", "Exp": "lnexp"}


def _estimate(eng, fn):
    fe = _FakeEng()
    fn(fe)
    name, a, k = fe.call
    if name == "matmul":
        n = _fsz(k["rhs"])
        return max(0.056, 0.035 + n * 0.00042), 0.1, None
    if name == "transpose":
        return 0.135, 0.1, None
    if name == "dma_start":
        src = k["in_"]
        nbytes = 128 * _fsz(k["out"]) * 4
        return (1.0 if eng == "pool" else 0.15), 2.0 + nbytes / 330000.0, None
    if name == "nop":
        return 0.05, 0.0, None
    if name == "activation":
        n = _fsz(k["out"])
        fname = str(k["func"]).split(".")[-1]
        return 0.2 + 0.00078 * n, 0.05, _TABS.get(fname)
    out = k.get("out", a[0] if a else None)
    n = _fsz(out) if out is not None else 1
    if eng == "pool":
        return 0.15 + 0.00185 * n, 0.05, None
    if name == "tensor_tensor_scan":
        return 0.08 + 0.0022 * n, 0.05, None
    if name == "reciprocal":
        return 0.12 + 0.0022 * n, 0.05, None
    if name in ("memset",):
        return 0.05 + 0.0003 * n, 0.05, None
    if name in ("bn_stats", "bn_aggr"):
        return 0.3, 0.05, None
    if name == "tensor_copy":
        return 0.1 + 0.0008 * n, 0.05, None
    src = k.get("in0", k.get("in_", None))
    if src is not None and _is_psum(src):
        return 0.12 + 0.00146 * n, 0.05, None
    return 0.12 + 0.00129 * n, 0.05, None


class Op:
    __slots__ = ("id", "eng", "fn", "kind", "slot", "deps", "dur", "lat", "tab", "prio", "start", "fin", "seq",
                 "nsucc", "rank")


class Tracker:
    ENG = ("pe", "act", "dve", "pool", "sp")

    def __init__(self):
        self.oplist = []
        self.last_w = {}
        self.readers = {}
        self.last_dma = {}
        self.barrier_id = None
        self.since_barrier = []
        self.rank = 0

    def _mk(self, eng, fn, kind, slot, reads, writes):
        extra = [k for k in reads if isinstance(k, tuple) and k[0] in ("bank", "psT")]
        if extra:
            writes = list(writes) + [k for k in extra if k not in writes]
        deps = {}
        for k in reads:
            w = self.last_w.get(k)
            if w is not None:
                deps[w] = True
        for k in writes:
            w = self.last_w.get(k)
            if w is not None:
                deps.setdefault(w, False)
            for r in self.readers.get(k, ()):
                deps.setdefault(r, False)
        if self.barrier_id is not None:
            deps.setdefault(self.barrier_id, False)
        if kind == "dma" and slot in self.last_dma:
            deps[self.last_dma[slot]] = True
        op = Op()
        op.id = len(self.oplist)
        op.eng, op.fn, op.kind, op.slot, op.deps = eng, fn, kind, slot, deps
        op.rank = self.rank
        if fn is None:
            op.dur, op.lat, op.tab = 0.05, 0.0, None
        else:
            op.dur, op.lat, op.tab = _estimate(eng, fn)
        self.oplist.append(op)
        self.since_barrier.append(op.id)
        for k in writes:
            self.last_w[k] = op.id
            self.readers[k] = set()
        for k in reads:
            self.readers.setdefault(k, set()).add(op.id)
        if kind == "dma":
            self.last_dma[slot] = op.id
        return op

    def op(self, e, fn, reads=(), writes=()):
        self._mk(e, fn, "eng", None, reads, writes)

    def dma(self, q, fn, slot, reads=(), writes=()):
        self._mk(q, fn, "dma", slot, reads, writes)

    def barrier(self):
        op = Op()
        op.id = len(self.oplist)
        op.eng, op.fn, op.kind, op.slot = "virt", None, "virt", None
        op.deps = {i: True for i in self.since_barrier}
        op.rank = self.rank
        op.dur, op.lat, op.tab = 0.0, 0.0, None
        self.oplist.append(op)
        self.barrier_id = op.id
        self.since_barrier = [op.id]
        self.last_w = {}
        self.readers = {}

    def wait_all_dma(self, e):
        self.barrier()
        self.final_barrier = self.barrier_id

    def mark(self, name):
        self.marks = getattr(self, "marks", [])
        self.marks.append((name, len(self.oplist)))

    def schedule(self):
        import heapq
        stop = os.environ.get("STOP_AFTER")
        if stop:
            k = int(stop)
            self.oplist = [o for o in self.oplist if o.id < k]
            fb = Op()
            fb.id = len(self.oplist)
            fb.eng, fb.fn, fb.kind, fb.slot = "virt", None, "virt", None
            fb.deps = {o.id: True for o in self.oplist}
            fb.dur, fb.lat, fb.tab, fb.rank = 0.0, 0.0, None, 10 ** 6
            self.oplist.append(fb)
            self.final_barrier = fb.id
        ops = self.oplist
        n = len(ops)
        succ = [[] for _ in range(n)]
        for o in ops:
            for d in o.deps:
                succ[d].append(o.id)
        for o in reversed(ops):
            best = 0.0
            for s_ in succ[o.id]:
                p = ops[s_].prio
                if p > best:
                    best = p
            o.prio = best + o.dur + o.lat
        for o in ops:
            o.prio = o.prio - 1.0e5 * o.rank
        if os.environ.get("PROG_ORDER"):
            for o in ops:
                o.prio = -float(o.id)
        indeg = [len(o.deps) for o in ops]
        ready_t = [0.0] * n
        free = {e: 0.0 for e in self.ENG}
        cur_tab = [None]
        heapA = {e: [] for e in self.ENG}
        heapB = {e: [] for e in self.ENG}
        order = {e: [] for e in self.ENG}
        done = 0

        def release(i):
            nonlocal done
            stack = [i]
            while stack:
                j = stack.pop()
                oj = ops[j]
                if oj.eng == "virt":
                    oj.start = ready_t[j]
                    oj.fin = ready_t[j]
                    done += 1
                    for s2 in succ[j]:
                        if oj.fin > ready_t[s2]:
                            ready_t[s2] = oj.fin
                        indeg[s2] -= 1
                        if indeg[s2] == 0:
                            stack.append(s2)
                else:
                    heapq.heappush(heapA[oj.eng], (ready_t[j], -oj.prio, j))

        for o in ops:
            if indeg[o.id] == 0:
                release(o.id)
        while done < n:
            best_e, best_t = None, None
            for e in self.ENG:
                A, Bh = heapA[e], heapB[e]
                while A and A[0][0] <= free[e] + 1e-9:
                    rt, np_, i = heapq.heappop(A)
                    heapq.heappush(Bh, (np_, i))
                if Bh:
                    t = free[e]
                elif A:
                    t = A[0][0]
                else:
                    continue
                if best_t is None or t < best_t:
                    best_e, best_t = e, t
            e = best_e
            if heapB[e]:
                if e == "act" and cur_tab[0] is not None and len(heapB[e]) > 1:
                    cands = heapq.nsmallest(6, heapB[e])
                    pick = cands[0]
                    if ops[pick[1]].tab not in (None, cur_tab[0]):
                        for c in cands[1:]:
                            if ops[c[1]].tab in (None, cur_tab[0]) and -c[0] > -pick[0] - 12.0:
                                pick = c
                                break
                    heapB[e].remove(pick)
                    heapq.heapify(heapB[e])
                    np_, i = pick
                else:
                    np_, i = heapq.heappop(heapB[e])
            else:
                rt, np_, i = heapq.heappop(heapA[e])
            o = ops[i]
            st_ = max(free[e], ready_t[i])
            extra = 0.0
            if e == "act" and o.tab is not None and o.tab != cur_tab[0]:
                extra = 1.3
                cur_tab[0] = o.tab
            o.start = st_
            free[e] = st_ + extra + o.dur
            o.fin = free[e] + o.lat
            order[e].append(i)
            done += 1
            for s_ in succ[i]:
                if o.fin > ready_t[s_]:
                    ready_t[s_] = o.fin
                indeg[s_] -= 1
                if indeg[s_] == 0:
                    release(s_)
        self.order = order
        self.makespan = max(o.fin for o in ops)
        pos = {}
        for e in self.ENG:
            for k_, i in enumerate(order[e]):
                pos[i] = k_
        marked = set()
        for o in ops:
            if o.kind == "virt":
                last = {}
                for d in o.deps:
                    po = ops[d]
                    if po.kind == "eng":
                        if po.eng not in last or pos[d] > pos[last[po.eng]]:
                            last[po.eng] = d
                marked.update(last.values())
                continue
            last = {}
            for d, raw in o.deps.items():
                po = ops[d]
                if po.kind != "eng":
                    continue
                if po.eng == o.eng and (o.eng == "pe" or not (raw or FULL_SYNC)):
                    continue
                if po.eng not in last or pos[d] > pos[last[po.eng]]:
                    last[po.eng] = d
            marked.update(last.values())
        if not SPARSE_INC:
            marked = set(o.id for o in ops if o.kind == "eng")
        self.marked = marked
        self.dma_cnt = {}
        for e in self.ENG:
            c = 0
            for i in order[e]:
                o = ops[i]
                if o.kind == "eng":
                    if i in marked:
                        c += 1
                        o.seq = c
                    else:
                        o.seq = None
        for o in ops:
            if o.kind == "dma":
                self.dma_cnt[o.slot] = self.dma_cnt.get(o.slot, 0) + 1
                o.seq = 16 * self.dma_cnt[o.slot]
        self.ops = {e: [] for e in self.ENG}
        vneed = {}
        for o in ops:
            if o.kind == "virt":
                nd = {}
                for d in o.deps:
                    po = ops[d]
                    if po.kind == "virt":
                        for dom, v in vneed[d].items():
                            if v > nd.get(dom, 0):
                                nd[dom] = v
                        continue
                    dom = ("dma", po.slot) if po.kind == "dma" else po.eng
                    if po.seq is not None and po.seq > nd.get(dom, 0):
                        nd[dom] = po.seq
                vneed[o.id] = nd
        for e in self.ENG:
            known = {}
            for i in order[e]:
                o = ops[i]
                need = {}
                for d, raw in o.deps.items():
                    po = ops[d]
                    if po.kind == "virt":
                        for dom, v in vneed[d].items():
                            if dom != e and v > need.get(dom, 0):
                                need[dom] = v
                        continue
                    if po.kind == "dma":
                        dom = ("dma", po.slot)
                    else:
                        dom = po.eng
                        if dom == e and (e == "pe" or not raw) and not (FULL_SYNC and e != "pe"):
                            continue
                    if po.seq is not None and po.seq > need.get(dom, 0):
                        need[dom] = po.seq
                waits = []
                for dom, v in need.items():
                    if v > known.get(dom, 0):
                        known[dom] = v
                        waits.append((dom, v))
                if o.kind == "eng":
                    self.ops[e].append((waits, o.fn, ("eng", e) if o.id in marked else ("noinc", e)))
                else:
                    self.ops[e].append((waits, o.fn, ("dma", o.slot)))
        fb = getattr(self, "final_barrier", None)
        if fb is not None:
            self.ops["sp"].append(([(dom, v) for dom, v in vneed[fb].items()], None, None))


def build(nlayers=2, debug_out=None):
    nc = bass.Bass("TRN2", target_bir_lowering=False)
    L = 2
    x_d = nc.dram_tensor("x", [S, D], F32, kind="ExternalInput").ap()
    w_in_d = nc.dram_tensor("w_in", [L, D, DIN], F32, kind="ExternalInput").ap()
    w_out_d = nc.dram_tensor("w_out", [L, D, D], F32, kind="ExternalInput").ap()
    w_up_d = nc.dram_tensor("w_up", [L, D, 2 * DFF], F32, kind="ExternalInput").ap()
    w_down_d = nc.dram_tensor("w_down", [L, DFF, D], F32, kind="ExternalInput").ap()
    sgu_w_d = nc.dram_tensor("sgu_w", [L, 4, 128, 128], F32, kind="ExternalInput").ap()
    pvec_d = nc.dram_tensor("pvec", [L, 128, NPV], F32, kind="ExternalInput").ap()
    rows_d = nc.dram_tensor("rows", [L, NRW], F32, kind="ExternalInput").ap()
    gfin_d = nc.dram_tensor("g_final", [D], F32, kind="ExternalInput").ap()
    cb_d = nc.dram_tensor("cbf", [128, 256], BF16, kind="ExternalInput").ap()
    cf_d = nc.dram_tensor("cf32", [128, 256], F32, kind="ExternalInput").ap()
    out_d = nc.dram_tensor("out", [S, D], F32, kind="ExternalOutput").ap()

    off = [16512]

    def alloc(name, shape, dt, at=None):
        nbytes = int(np.prod(shape[1:])) * (4 if dt == F32 else 2)
        nbytes = (nbytes + 31) // 32 * 32
        if at is None:
            o = off[0]
            off[0] += nbytes
        else:
            o = at
        assert o + nbytes <= 229344, (name, o, nbytes)
        return nc.alloc_sbuf_tensor_at(name, list(shape), dt, offset=o), o + nbytes

    x_sb, _ = alloc("x_sb", [128, NT, D], F32)
    cbf, _ = alloc("cbf_sb", [128, 256], BF16)
    cf32, _ = alloc("cf32_sb", [128, 256], F32)
    pvec, _ = alloc("pvec_sb", [128, NPV], F32)
    wsT, _ = alloc("wsT", [128, 4, 128], BF16)
    misc, _ = alloc("misc", [128, 64], F32)
    gbc, _ = alloc("gbc", [128, D], F32)
    goutbc, _ = alloc("goutbc", [128, 768], F32)
    lngbc, _ = alloc("lngbc", [128, 256], F32)
    lnbbc, _ = alloc("lnbbc", [128, 256], F32)
    bsb, _ = alloc("bsb", [128, 256], F32)
    hb = [alloc("hb%d" % i, [128, D], BF16)[0] for i in range(2)]
    junk, _je = alloc("junk", [128, 512], BF16)
    ycn = junk
    wtmp, _ = alloc("wtmp", [128, 128], F32, at=_je - 1024)
    wtmpb, _ = alloc("wtmpb", [128, 128], BF16, at=_je - 512)
    hT, _ = alloc("hT", [128, 8, 512], BF16)
    wbA = []
    wbB = []
    for i in range(2):
        t, _ = alloc("wbA%d" % i, [128, 8, 512], BF16)
        wbA.append(t)
        wbB.append(alloc("wbB%d" % i, [128, 8, 2, 256], BF16, at=off[0] - 8192)[0])
    B = off[0]
    kT, _ = alloc("kT", [128, 4, S], BF16)
    v_sb, _ = alloc("v_sb", [128, NT, 512], BF16)
    qTs = [alloc("qT%d" % i, [128, 4, 512], BF16)[0] for i in range(2)]
    yTab = [alloc("yTab%d" % i, [128, 4, 512], BF16)[0] for i in range(2)]
    yTc, _ = alloc("yTc", [128, 4, 512], BF16)
    SC = off[0]
    hc, e1 = alloc("hc", [128, 2, 542], F32, at=SC)
    acc, e1 = alloc("acc", [128, 2, 512], F32, at=e1)
    yc, _ = alloc("yc", [128, 4, 512], F32, at=e1 - 4096)
    sq, e1 = alloc("sq", [128, 2, 512], F32, at=e1)
    st = []
    t, e1 = alloc("st0", [128, 512], F32, at=e1)
    st.append(t)
    ug, e2 = alloc("ug", [128, 2, 256], F32, at=e1)
    vg, e2 = alloc("vg", [128, 2, 256], F32, at=e2)
    stmp, e2 = alloc("stmp", [128, 256], F32, at=e2)
    vn, e2 = alloc("vn", [128, 256], BF16, at=e2)
    yb = ug
    ybn, e2 = alloc("ybn", [128, 256], BF16, at=e2)
    off[0] = e2
    GW = 2052
    G = [alloc("G%d" % i, [128, GW], F32)[0] for i in range(2)]
    A = [alloc("A%d" % i, [128, S], BF16)[0] for i in range(2)]
    AT, _ = alloc("AT", [128, S], BF16)
    mixer_end = off[0]
    off[0] = B
    wd, _ = alloc("wd", [128, 22, D], BF16)
    actb = [alloc("actb%d" % i, [128, 22, 512], BF16)[0] for i in range(2)]
    ga = [alloc("ga%d" % i, [128, 512], F32)[0] for i in range(2)]
    va = [alloc("va%d" % i, [128, 512], F32)[0] for i in range(2)]
    halo, _ = alloc("halo", [128, 2, 44, 4], F32)
    ffn_end = off[0]

    identb = cbf[:, 0:128]
    maskL = cbf[:, 128:256]
    ones32 = cf32[:, 0:128]
    tril = cf32[:, 128:256]

    psT = nc.alloc_psum_tensor("psT", [128, 2048], BF16)
    psTf = psT.bitcast(F32)
    psM = nc.alloc_psum_tensor("psM", [128, 1024], F32)
    psZ = nc.alloc_psum_tensor("psZ", [128, 1024], F32)
    psW = nc.alloc_psum_tensor("psW", [128, 1024], F32)

    def bank(b):
        if b < 2:
            return psM[:, b * 512:(b + 1) * 512]
        if b < 4:
            return psZ[:, (b - 2) * 512:(b - 1) * 512]
        if b < 6:
            return psW[:, (b - 4) * 512:(b - 3) * 512]
        return psTf[:, (b - 6) * 512:(b - 5) * 512]

    def bkey(b):
        if b >= 6:
            return ("psT", b - 6)
        return ("bank", b)

    T = Tracker()
    rot = {"i": 0}

    def next_bank(pool):
        rot["i"] += 1
        return pool[rot["i"] % len(pool)]

    def pv(col):
        return pvec[:, col:col + 1]

    wjobs = []
    wstate = {"issued": 0}

    def issue_job(n):
        kind, l, a = wjobs[n]
        s = n % 2
        if kind == "in":
            src = w_in_d[l].rearrange("(k p) c -> p k c", p=128)[:, :, a:a + 512]
            dst = wbA[s]
            T.dma("pool", lambda e, dst=dst, src=src: e.dma_start(out=dst[:, :, :], in_=src),
                  "wb%d" % s, writes=[("wb", s), ("wbv", s)])
        elif kind == "out":
            src = w_out_d[l].rearrange("(k p) c -> p k c", p=128)[:, :, a:a + 512]
            dst = wbA[s]
            T.dma("pool", lambda e, dst=dst, src=src: e.dma_start(out=dst[:, :, :], in_=src),
                  "wb%d" % s, writes=[("wb", s), ("wbv", s)])
        elif kind == "up":
            for which in range(2):
                src = w_up_d[l].rearrange("(k p) c -> p k c", p=128)[:, :, which * DFF + a:which * DFF + a + 256]
                dst = wbB[s]
                T.dma("pool", lambda e, dst=dst, src=src, which=which: e.dma_start(out=dst[:, :, which, :], in_=src),
                      ("wb%d" if which == 0 else "wv%d") % s, writes=[("wb" if which == 0 else "wbv", s)])

    def get_w(n):
        while wstate["issued"] <= min(n + 1, len(wjobs) - 1):
            issue_job(wstate["issued"])
            wstate["issued"] += 1
        return n % 2

    for l in range(nlayers):
        def _ja():
            for ch_ in ORDER:
                if ch_ != "c":
                    wjobs.append(("in", l, {"a": 0, "k": 1536, "v": 2048, "q": 1024, "b": 512}[ch_]))

        def _jb():
            for c0 in (0, 512):
                wjobs.append(("out", l, c0))

        _ja()
        for J in range(4):
            if J + 1 < 4:
                _ja()
            _jb()
        for J in range(4):
            for gg in range(11):
                wjobs.append(("up", l, gg * 256))
    wj = {"n": 0}

    def take_w():
        n = wj["n"]
        wj["n"] += 1
        return get_w(n)

    for q in range(4):
        src = x_d.rearrange("(i p) d -> p i d", p=128)[:, 4 * q:4 * q + 4, :]
        T.dma("sp", lambda e, q=q, src=src: e.dma_start(out=x_sb[:, 4 * q:4 * q + 4, :], in_=src),
              "x%d" % q, writes=[("x", 4 * q + j) for j in range(4)])
    T.dma("sp", lambda e: e.dma_start(out=cbf[:, :], in_=cb_d), "c0", writes=["cbf"])
    T.dma("sp", lambda e: e.dma_start(out=cf32[:, :], in_=cf_d), "c1", writes=["cf32"])

    def load_bc(dst, src_row, key, slot):
        T.dma("sp", lambda e, dst=dst, src_row=src_row: e.dma_start(out=dst, in_=src_row.partition_broadcast(128)),
              slot, writes=[key])

    def emit_rstd(ssap, n, scale, key):
        T.op("dve", lambda e: e.tensor_scalar(out=ssap, in0=ssap, scalar1=scale, scalar2=EPS, op0=ALU.mult, op1=ALU.add),
             reads=[key], writes=[key])
        T.op("act", lambda e: e.activation(out=ssap, in_=ssap, func=AF.Sqrt), reads=[key], writes=[key])
        T.op("dve", lambda e: e.reciprocal(out=ssap, in_=ssap), reads=[key], writes=[key])

    def emit_norm_to_hT(J):
        ss = misc[:, 0:4]
        T.op("dve", lambda e: e.memset(ss, 0.0), writes=["ss"])
        T.op("dve", lambda e: e.memset(misc[:, 12:16], 0.0), writes=["ss"])
        for j in range(4):
            i = 4 * J + j
            for hf in range(2):
                T.op("act", lambda e, i=i, j=j, hf=hf: e.activation(
                    out=junk[:, :], in_=x_sb[:, i, hf * 512:(hf + 1) * 512], func=AF.Square,
                    accum_out=misc[:, 12 * hf + j:12 * hf + j + 1]),
                    reads=[("x", i), "ss"], writes=["junk", ("ssj", j, hf)])
        T.op("dve", lambda e: e.tensor_tensor(out=misc[:, 4:8], in0=misc[:, 0:4], in1=misc[:, 12:16], op=ALU.add),
             reads=["ss"] + [("ssj", j, hf) for j in range(4) for hf in range(2)], writes=["rs"])
        emit_rstd(misc[:, 4:8], 4, 1.0 / D, "rs")
        for j in range(4):
            i = 4 * J + j
            b = j % 2
            T.op("dve", lambda e, i=i, j=j, b=b: e.scalar_tensor_tensor(
                out=hb[b][:, :], in0=x_sb[:, i, :], scalar=misc[:, 4 + j:5 + j], in1=gbc[:, :],
                op0=ALU.mult, op1=ALU.mult), reads=[("x", i), "rs", "gbc"], writes=[("hb", b)])
            for k in range(8):
                T.op("pe", lambda e, k=k, b=b: e.transpose(psT[:, b * 1024 + k * 128: b * 1024 + (k + 1) * 128],
                                                          hb[b][:, k * 128:(k + 1) * 128], identb),
                     reads=[("hb", b), "cbf"], writes=[("psT", b)])
            T.op("act", lambda e, j=j, b=b: e.activation(
                out=hT[:, :, j * 128:(j + 1) * 128],
                in_=psT[:, b * 1024:(b + 1) * 1024].rearrange("p (k t) -> p k t", k=8), func=AF.Copy),
                reads=[("psT", b)], writes=[("hT", j)])

    hT_keys = [("hT", j) for j in range(4)]
    WIDE = [0, 4, 5]
    GEN = [0, 4, 5]

    def mm_group_fm(s, fc, b, wview):
        for k in range(8):
            T.op("pe", lambda e, k=k, fc=fc, b=b, s=s: e.matmul(bank(b), lhsT=wview[s][:, k, fc * 128:(fc + 1) * 128],
                                                               rhs=hT[:, k, :], start=(k == 0), stop=(k == 7)),
                 reads=[("wb", s), ("wbv", s)] + hT_keys, writes=[bkey(b)])

    def mm_group_tm(s, j, b):
        for k in range(8):
            T.op("pe", lambda e, k=k, j=j, b=b, s=s: e.matmul(bank(b), lhsT=hT[:, k, j * 128:(j + 1) * 128],
                                                             rhs=wbA[s][:, k, :], start=(k == 0), stop=(k == 7)),
                 reads=[("wb", s), ("wbv", s), ("hT", j)], writes=[bkey(b)])

    def emit_layer_setup(l):
        T.dma("sp", lambda e, l=l: e.dma_start(out=pvec[:, :], in_=pvec_d[l]), "pv", writes=["pvec"])
        load_bc(goutbc[:, :], rows_d[l, RW_GOUT:RW_GOUT + 768], "goutbc", "r0")
        load_bc(lngbc[:, :], rows_d[l, RW_LNG:RW_LNG + 256], "lngbc", "r1")
        load_bc(lnbbc[:, :], rows_d[l, RW_LNB:RW_LNB + 256], "lnbbc", "r2")
        T.op("dve", lambda e: e.memset(bsb[:, :], 0.0), writes=["bsb"])
        for h in range(4):
            T.op("dve", lambda e, h=h: e.tensor_scalar_add(out=bsb[:, 64 * h:64 * h + 64], in0=bsb[:, 64 * h:64 * h + 64],
                                                           scalar1=pv(PV_SB + h)), reads=["pvec", "bsb"], writes=["bsb"])
        for h in range(4):
            T.dma("sp", lambda e, l=l, h=h: e.dma_start(out=wtmp[:, :], in_=sgu_w_d[l, h]), "ws", writes=["wtmp", "junk"])
            T.op("dve", lambda e: e.tensor_tensor(out=wtmpb[:, :], in0=wtmp[:, :], in1=tril, op=ALU.mult),
                 reads=["wtmp", "cf32", "junk"], writes=["wtmpb", "junk"])
            T.op("pe", lambda e: e.transpose(psT[:, 0:128], wtmpb[:, :], identb), reads=["wtmpb", "cbf", "junk"],
                 writes=[("psT", 0)])
            T.op("act", lambda e, h=h: e.activation(out=wsT[:, h, :], in_=psT[:, 0:128], func=AF.Copy),
                 reads=[("psT", 0)], writes=["wsT"])
        for b in range(2):
            T.op("dve", lambda e, b=b: e.memset(G[b][:, 2048:2049], 1.0), writes=[("G", b)])

    def emit_win(l, J, letters="akvq"):
        evq = {"i": 0}

        def cp(out, in_, reads, writes, scale=None):
            evq["i"] += 1
            if scale is not None or evq["i"] % 2 == 0:
                if scale is None:
                    T.op("act", lambda e: e.activation(out=out, in_=in_, func=AF.Copy), reads=reads, writes=writes)
                else:
                    T.op("act", lambda e: e.activation(out=out, in_=in_, func=AF.Copy, scale=scale), reads=reads,
                         writes=writes)
            else:
                T.op("dve", lambda e: e.tensor_copy(out=out, in_=in_), reads=reads, writes=writes)

        def grp_a():
            s = take_w()
            if J == 0:
                T.op("dve", lambda e: e.memset(hc[:, :, 0:30], 0.0), writes=["hc"])
            else:
                T.op("dve", lambda e: e.tensor_copy(out=hc[:, :, 0:30], in_=hc[:, :, 512:542]), reads=["hc"],
                     writes=["hc"])
            for c in range(2):
                b = next_bank(WIDE)
                mm_group_fm(s, 2 + c, b, wbA)
                T.op("act", lambda e, c=c, b=b: e.activation(out=sq[:, c, :], in_=bank(b), func=AF.Sigmoid),
                     reads=[bkey(b)], writes=[("sq", c)])
            for c in range(2):
                b = next_bank(WIDE)
                mm_group_fm(s, c, b, wbA)
                T.op("dve", lambda e, c=c, b=b: e.tensor_tensor(out=hc[:, c, 30:542], in0=bank(b), in1=sq[:, c, :],
                                                                op=ALU.mult),
                     reads=[bkey(b), ("sq", c), "hc"], writes=["hc"])

        def grp_k():
            s = take_w()
            for fc in range(4):
                b = next_bank(WIDE)
                mm_group_fm(s, fc, b, wbA)
                cp(kT[:, fc, J * 512:(J + 1) * 512], bank(b), [bkey(b)], [("kT", J)])

        def grp_v():
            s = take_w()
            for j in range(4):
                b = next_bank(WIDE)
                mm_group_tm(s, j, b)
                cp(v_sb[:, 4 * J + j, :], bank(b), [bkey(b)], [("v", 4 * J + j)])

        def grp_q():
            s = take_w()
            for fc in range(4):
                b = next_bank(WIDE)
                mm_group_fm(s, fc, b, wbA)
                cp(qTs[J % 2][:, fc, :], bank(b), [bkey(b)], [("qT", J % 2)], scale=0.125)

        for ch_ in letters:
            {"a": grp_a, "k": grp_k, "v": grp_v, "q": grp_q}[ch_]()

    def emit_conv(J):
        for k in range(31):
            for c in range(2):
                if k == 0:
                    T.op("dve", lambda e, c=c: e.tensor_scalar(out=acc[:, c, :], in0=hc[:, c, 0:512],
                                                               scalar1=pv(PV_CW + c * 31), scalar2=pv(PV_CB + c),
                                                               op0=ALU.mult, op1=ALU.add),
                         reads=["hc", "pvec"], writes=[("acc", c)])
                else:
                    T.op("dve", lambda e, c=c, k=k: e.scalar_tensor_tensor(
                        out=acc[:, c, :], in0=hc[:, c, k:k + 512], scalar=pv(PV_CW + c * 31 + k), in1=acc[:, c, :],
                        op0=ALU.mult, op1=ALU.add), reads=["hc", ("acc", c)], writes=[("acc", c)])
        T.op("act", lambda e: e.activation(out=sq[:, :, :], in_=acc[:, :, :], func=AF.Square),
             reads=[("acc", 0), ("acc", 1)], writes=[("sq", 0), ("sq", 1)])
        b1 = next_bank(GEN)
        for c in range(2):
            T.op("pe", lambda e, c=c, b1=b1: e.matmul(bank(b1), lhsT=ones32, rhs=acc[:, c, :], start=(c == 0),
                                                      stop=(c == 1)), reads=[("acc", c), "cf32"], writes=[bkey(b1)])
        T.op("act", lambda e, b1=b1: e.activation(out=st[0][:, :], in_=bank(b1), func=AF.Copy, scale=1.0 / 256),
             reads=[bkey(b1)], writes=["st0"])
        b2 = next_bank(GEN)
        for c in range(2):
            T.op("pe", lambda e, c=c, b2=b2: e.matmul(bank(b2), lhsT=ones32, rhs=sq[:, c, :], start=(c == 0),
                                                      stop=(c == 1)), reads=[("sq", c), "cf32"], writes=[bkey(b2)])
        T.op("dve", lambda e: e.tensor_tensor(out=sq[:, 1, :], in0=st[0][:, :], in1=st[0][:, :], op=ALU.mult),
             reads=["st0"], writes=[("sq", 1)])
        T.op("dve", lambda e, b2=b2: e.scalar_tensor_tensor(out=sq[:, 1, :], in0=bank(b2), scalar=1.0 / 256,
                                                            in1=sq[:, 1, :], op0=ALU.mult, op1=ALU.subtract),
             reads=[bkey(b2), ("sq", 1)], writes=[("sq", 1)])
        T.op("dve", lambda e: e.tensor_scalar_add(out=sq[:, 1, :], in0=sq[:, 1, :], scalar1=EPS), reads=[("sq", 1)],
             writes=[("sq", 1)])
        T.op("act", lambda e: e.activation(out=sq[:, 1, :], in_=sq[:, 1, :], func=AF.Ln), reads=[("sq", 1)],
             writes=[("sq", 1)])
        T.op("act", lambda e: e.activation(out=sq[:, 1, :], in_=sq[:, 1, :], func=AF.Exp, scale=-0.5), reads=[("sq", 1)],
             writes=[("sq", 1)])
        for c in range(2):
            T.op("dve", lambda e, c=c: e.tensor_tensor(out=acc[:, c, :], in0=acc[:, c, :], in1=st[0][:, :],
                                                       op=ALU.subtract), reads=[("acc", c), "st0"], writes=[("acc", c)])
        for c in range(2):
            T.op("dve", lambda e, c=c: e.tensor_tensor(out=acc[:, c, :], in0=acc[:, c, :], in1=sq[:, 1, :],
                                                       op=ALU.mult), reads=[("acc", c), ("sq", 1)], writes=[("acc", c)])
        for c in range(2):
            T.op("act", lambda e, c=c: e.activation(out=acc[:, c, :], in_=acc[:, c, :], func=AF.Silu,
                                                    scale=pv(PV_LG + c), bias=pv(PV_LB + c)),
                 reads=[("acc", c), "pvec"], writes=[("acc", c)])
        T.op("act", lambda e: e.activation(out=sq[:, :, :], in_=acc[:, :, :], func=AF.Square),
             reads=[("acc", 0), ("acc", 1)], writes=[("sq", 0), ("sq", 1)])
        b3 = next_bank(GEN)
        for c in range(2):
            T.op("pe", lambda e, c=c, b3=b3: e.matmul(bank(b3), lhsT=ones32, rhs=sq[:, c, :], start=(c == 0),
                                                      stop=(c == 1)), reads=[("sq", c), "cf32"], writes=[bkey(b3)])
        T.op("dve", lambda e, b3=b3: e.tensor_scalar(out=st[0][:, :], in0=bank(b3), scalar1=1.0 / 256, scalar2=EPS,
                                                     op0=ALU.mult, op1=ALU.add), reads=[bkey(b3)], writes=["st0"])
        T.op("act", lambda e: e.activation(out=st[0][:, :], in_=st[0][:, :], func=AF.Ln), reads=["st0"],
             writes=["st0"])
        T.op("act", lambda e: e.activation(out=st[0][:, :], in_=st[0][:, :], func=AF.Exp, scale=-0.5), reads=["st0"],
             writes=["st0"])
        for c in range(2):
            T.op("dve", lambda e, c=c: e.scalar_tensor_tensor(out=yTab[J % 2][:, c, :], in0=acc[:, c, :], scalar=pv(PV_GA + c),
                                                              in1=st[0][:, :], op0=ALU.mult, op1=ALU.mult),
                 reads=[("acc", c), "st0", "pvec"], writes=[("yT", J % 2, c)])

    def emit_sgu(J):
        s = take_w()
        for half in range(2):
            tiles = [2 * half, 2 * half + 1]
            for j in tiles:
                jl = j - 2 * half
                b = next_bank(GEN)
                mm_group_tm(s, j, b)
                T.op("act", lambda e, jl=jl, b=b: e.activation(out=ug[:, jl, :], in_=bank(b)[:, 0:256], func=AF.Gelu),
                     reads=[bkey(b)], writes=[("ug", jl)])
                T.op("act", lambda e, jl=jl, b=b: e.activation(out=vg[:, jl, :], in_=bank(b)[:, 256:512], func=AF.Gelu),
                     reads=[bkey(b)], writes=[("vg", jl)])
            emit_sgu_half(J, half)

    def emit_sgu_half(J, half):
        j0 = 2 * half
        rs4 = misc[:, 48 + j0:50 + j0]
        ssb = misc[:, 52 + j0:54 + j0]
        rsb = misc[:, 56 + j0:58 + j0]
        for jl in range(2):
            j = j0 + jl
            T.op("dve", lambda e, j=j, jl=jl: e.bn_stats(out=misc[:, 16 + 6 * j:22 + 6 * j], in_=vg[:, jl, :]),
                 reads=[("vg", jl)], writes=[("bst", j)])
            T.op("dve", lambda e, j=j: e.bn_aggr(out=misc[:, 40 + 2 * j:42 + 2 * j], in_=misc[:, 16 + 6 * j:22 + 6 * j]),
                 reads=[("bst", j)], writes=[("mv", j)])
        T.op("dve", lambda e: e.tensor_scalar_add(
            out=rs4, in0=misc[:, 40 + 2 * j0:44 + 2 * j0].rearrange("p (j two) -> p j two", two=2)[:, :, 1], scalar1=EPS),
            reads=[("mv", j0), ("mv", j0 + 1)], writes=[("rs4", half)])
        T.op("act", lambda e: e.activation(out=rs4, in_=rs4, func=AF.Sqrt), reads=[("rs4", half)], writes=[("rs4", half)])
        T.op("dve", lambda e: e.reciprocal(out=rs4, in_=rs4), reads=[("rs4", half)], writes=[("rs4", half)])
        T.op("dve", lambda e: e.memset(ssb, 0.0), writes=[("ssb", half)])
        for jl in range(2):
            j = j0 + jl
            T.op("dve", lambda e, j=j, jl=jl: e.tensor_scalar(out=stmp[:, :], in0=vg[:, jl, :],
                                                             scalar1=misc[:, 40 + 2 * j:41 + 2 * j],
                                                             scalar2=misc[:, 48 + j:49 + j], op0=ALU.subtract, op1=ALU.mult),
                 reads=[("vg", jl), ("mv", j), ("rs4", half)], writes=["stmp"])
            T.op("dve", lambda e: e.tensor_tensor(out=stmp[:, :], in0=stmp[:, :], in1=lngbc[:, :], op=ALU.mult),
                 reads=["stmp", "lngbc"], writes=["stmp"])
            T.op("dve", lambda e: e.tensor_tensor(out=vn[:, :], in0=stmp[:, :], in1=lnbbc[:, :], op=ALU.add),
                 reads=["stmp", "lnbbc"], writes=["vn"])
            b = next_bank(GEN)
            for h in range(4):
                T.op("pe", lambda e, h=h, b=b: e.matmul(bank(b)[:, 64 * h:64 * h + 64], lhsT=wsT[:, h, :],
                                                        rhs=vn[:, 64 * h:64 * h + 64], start=True, stop=True),
                     reads=["vn", "wsT"], writes=[bkey(b)])
            T.op("dve", lambda e, b=b: e.tensor_tensor(out=stmp[:, :], in0=bank(b)[:, 0:256], in1=bsb[:, :], op=ALU.add),
                 reads=[bkey(b), "bsb"], writes=["stmp"])
            T.op("dve", lambda e, jl=jl: e.tensor_tensor(out=yb[:, jl, :], in0=stmp[:, :], in1=ug[:, jl, :], op=ALU.mult),
                 reads=["stmp", ("ug", jl)], writes=[("ug", jl)])
            T.op("act", lambda e, j=j, jl=jl: e.activation(out=junk[:, 0:256], in_=yb[:, jl, :], func=AF.Square,
                                                           accum_out=misc[:, 52 + j:53 + j]),
                 reads=[("ug", jl), ("ssb", half)], writes=["junk", ("ssbj", j)])
        T.op("dve", lambda e: e.tensor_copy(out=rsb, in_=ssb), reads=[("ssb", half), ("ssbj", j0), ("ssbj", j0 + 1)],
             writes=[("rsb", half)])
        emit_rstd(rsb, 2, 1.0 / 256, ("rsb", half))
        for jl in range(2):
            j = j0 + jl
            T.op("dve", lambda e, j=j, jl=jl: e.scalar_tensor_tensor(out=ybn[:, :], in0=yb[:, jl, :],
                                                                    scalar=misc[:, 56 + j:57 + j],
                                                                    in1=goutbc[:, 0:256], op0=ALU.mult, op1=ALU.mult),
                 reads=[("ug", jl), ("rsb", half), "goutbc"], writes=["ybn"])
            for k in range(2):
                T.op("pe", lambda e, k=k: e.transpose(psT[:, 1024 + k * 128:1024 + (k + 1) * 128],
                                                      ybn[:, k * 128:(k + 1) * 128], identb),
                     reads=["ybn", "cbf"], writes=[("psT", 1)])
            T.op("act", lambda e, j=j: e.activation(out=yTab[J % 2][:, 2:4, j * 128:(j + 1) * 128],
                                                    in_=psT[:, 1024:1280].rearrange("p (k t) -> p k t", k=2),
                                                    func=AF.Copy), reads=[("psT", 1)], writes=[("yT", J % 2, 2), ("yT", J % 2, 3)])

    def emit_attention(J):
        units = [(j, h) for j in range(4) for h in range(8)]
        zkeys = [("bank", 2), ("bank", 3), ("bank", 4), ("bank", 5)]
        ssc = misc[:, 60:64]

        def st_z(ui):
            j, h = units[ui]
            i = 4 * J + j
            n = 128 * (i + 1)
            fc, po = h // 2, 64 * (h % 2)
            nb = (n + 511) // 512
            g = ui % 2
            for kb in range(nb):
                c0 = kb * 512
                w = min(512, n - c0)
                last = kb == nb - 1
                zb = kb % 2
                zk = ("bank", 2 + zb)
                T.op("pe", lambda e, c0=c0, w=w, last=last, fc=fc, po=po, j=j, zb=zb: e.matmul(
                    psZ[:, zb * 512:zb * 512 + w], lhsT=qTs[J % 2][po:po + 64, fc, j * 128:(j + 1) * 128],
                    rhs=kT[po:po + 64, fc, c0:c0 + w], start=True, stop=not last),
                    reads=[("qT", J % 2)] + [("kT", jj) for jj in range(J + 1)], writes=[zk])
                if last:
                    T.op("pe", lambda e, w=w, zb=zb: e.matmul(psZ[:, zb * 512 + w - 128:zb * 512 + w], lhsT=maskL,
                                                             rhs=identb, start=False, stop=True),
                         reads=["cbf"], writes=[zk])
                T.op("act", lambda e, n=n, g=g, c0=c0, w=w, zb=zb: e.activation(
                    out=G[g][:, 2048 - n + c0:2048 - n + c0 + w], in_=psZ[:, zb * 512:zb * 512 + w], func=AF.Sigmoid,
                    scale=-1.0), reads=[zk], writes=[("G", g)])

        def st_sig(ui):
            pass

        def st_scan(ui):
            j, h = units[ui]
            n = 128 * (4 * J + j + 1)
            g = ui % 2
            rev = bass.AP(G[g], 2047, [[GW, 128], [-1, n]])
            T.op("dve", lambda e, rev=rev: e.tensor_tensor_scan(out=rev, data0=rev, data1=rev, initial=1.0,
                                                                op0=ALU.mult, op1=ALU.bypass),
                 reads=[("G", g)], writes=[("G", g)])
            T.op("dve", lambda e, n=n, g=g: e.tensor_tensor(out=A[g][:, 0:n], in0=G[g][:, 2049 - n:2049],
                                                            in1=G[g][:, 2048 - n:2048], op=ALU.subtract),
                 reads=[("G", g)], writes=[("A", g)])

        def st_tr(ui):
            j, h = units[ui]
            i = 4 * J + j
            g = ui % 2
            for c in range(i + 1):
                T.op("pe", lambda e, c=c, g=g: e.transpose(psT[:, c * 128:(c + 1) * 128], A[g][:, c * 128:(c + 1) * 128],
                                                           identb),
                     reads=[("A", g), "cbf"], writes=[("psT", 1)] if c >= 8 else [("psT", 0)])

        def st_cp(ui):
            j, h = units[ui]
            n = 128 * (4 * J + j + 1)
            T.op("act", lambda e, n=n: e.activation(out=AT[:, 0:n], in_=psT[:, 0:n], func=AF.Copy),
                 reads=[("psT", 0), ("psT", 1)] if n > 1024 else [("psT", 0)], writes=["AT"])

        def st_av(ui):
            j, h = units[ui]
            i = 4 * J + j
            for c in range(i + 1):
                T.op("pe", lambda e, c=c, h=h, i=i: e.matmul(bank(1)[:, 64 * h:64 * h + 64],
                                                             lhsT=AT[:, c * 128:(c + 1) * 128],
                                                             rhs=v_sb[:, c, 64 * h:64 * h + 64], start=(c == 0),
                                                             stop=(c == i)),
                     reads=["AT", ("v", c)], writes=[("bank", 1)])
            if h == 7:
                T.op("act", lambda e, j=j: e.activation(out=yc[:, j, :], in_=bank(1), func=AF.Copy),
                     reads=[("bank", 1)], writes=[("yc", j), ("acc", 0), ("acc", 1), ("sq", 0), ("sq", 1)])
                T.op("act", lambda e, j=j: e.activation(out=junk[:, 0:512], in_=yc[:, j, :], func=AF.Square,
                                                        accum_out=misc[:, 60 + j:61 + j]),
                     reads=[("yc", j), "ssc", ("acc", 0), ("acc", 1), ("sq", 0), ("sq", 1)], writes=["junk", ("sscj", j)])

        T.op("dve", lambda e: e.memset(ssc, 0.0), writes=["ssc"])
        for ui in range(len(units)):
            for st_fn in (st_z, st_scan, st_tr, st_cp, st_av):
                st_fn(ui)
        T.op("dve", lambda e: e.tensor_copy(out=misc[:, 8:12], in_=ssc), reads=["ssc"] + [("sscj", j) for j in range(4)],
             writes=["rsc"])
        emit_rstd(misc[:, 8:12], 4, 1.0 / 512, "rsc")
        for j in range(4):
            T.op("dve", lambda e, j=j: e.scalar_tensor_tensor(out=ycn[:, :], in0=yc[:, j, :], scalar=misc[:, 8 + j:9 + j],
                                                              in1=goutbc[:, 256:768], op0=ALU.mult, op1=ALU.mult),
                 reads=[("yc", j), "rsc", "goutbc", ("acc", 0), ("acc", 1), ("sq", 0), ("sq", 1)], writes=["junk"])
            for k in range(4):
                T.op("pe", lambda e, k=k: e.transpose(psT[:, 1024 + k * 128:1024 + (k + 1) * 128],
                                                      ycn[:, k * 128:(k + 1) * 128], identb),
                     reads=["junk", "cbf"], writes=[("psT", 1)])
            T.op("act", lambda e, j=j: e.activation(out=yTc[:, 0:4, j * 128:(j + 1) * 128],
                                                    in_=psT[:, 1024:1536].rearrange("p (k t) -> p k t", k=4),
                                                    func=AF.Copy), reads=[("psT", 1)],
                 writes=[("yTc", 0)])

    def emit_wout(l, J):
        for gidx in range(2):
            s = take_w()
            for j in range(4):
                i = 4 * J + j
                b = next_bank(GEN)
                for k in range(8):
                    T.op("pe", lambda e, k=k, j=j, b=b, s=s: e.matmul(bank(b), lhsT=(yTab[J % 2][:, k, j * 128:(j + 1) * 128] if k < 4 else yTc[:, k - 4, j * 128:(j + 1) * 128]),
                                                                     rhs=wbA[s][:, k, :], start=(k == 0), stop=(k == 7)),
                         reads=[("wb", s), ("wbv", s), ("yTc", 0)] + [("yT", J % 2, kk) for kk in range(4)], writes=[bkey(b)])
                T.op("dve", lambda e, i=i, b=b, gidx=gidx: e.tensor_tensor(
                    out=x_sb[:, i, gidx * 512:(gidx + 1) * 512], in0=bank(b), in1=x_sb[:, i, gidx * 512:(gidx + 1) * 512],
                    op=ALU.add), reads=[bkey(b), ("x", i)], writes=[("x", i)])

    def emit_ffn(l, J):
        emit_norm_to_hT(J)
        ab = J % 2
        par = J % 2
        UPB = [[2, 3], [4, 0]]
        ui = 0
        for gg in range(11):
            s = take_w()
            for pi in range(2):
                g = 2 * gg + pi
                bs = UPB[ui % 2]
                ui += 1
                t = ui % 2
                for which in range(2):
                    b = bs[which]
                    for k in range(8):
                        T.op("pe", lambda e, k=k, b=b, s=s, which=which, pi=pi: e.matmul(
                            bank(b), lhsT=wbB[s][:, k, which, pi * 128:(pi + 1) * 128], rhs=hT[:, k, :],
                            start=(k == 0), stop=(k == 7)), reads=[("wb" if which == 0 else "wbv", s)] + hT_keys,
                            writes=[bkey(b)])
                for which in range(2):
                    b = bs[which]
                    ch = g + 22 * which
                    dst = ga[t] if which == 0 else va[t]
                    dk = ("ga", t) if which == 0 else ("va", t)
                    w0, w1, w2 = pv(PV_FW + 3 * ch), pv(PV_FW + 3 * ch + 1), pv(PV_FW + 3 * ch + 2)
                    T.op("act", lambda e, b=b, dst=dst, w2=w2, ch=ch: e.activation(
                        out=dst[:, :], in_=bank(b), func=AF.Identity, scale=w2, bias=pv(PV_FB + ch)),
                        reads=[bkey(b), "pvec"], writes=[dk])
                    T.op("dve", lambda e, b=b, dst=dst, w1=w1: e.scalar_tensor_tensor(
                        out=dst[:, 1:512], in0=bank(b)[:, 0:511], scalar=w1, in1=dst[:, 1:512], op0=ALU.mult,
                        op1=ALU.add), reads=[bkey(b), dk], writes=[dk])
                    T.op("dve", lambda e, b=b, dst=dst, w0=w0: e.scalar_tensor_tensor(
                        out=dst[:, 2:512], in0=bank(b)[:, 0:510], scalar=w0, in1=dst[:, 2:512], op0=ALU.mult,
                        op1=ALU.add), reads=[bkey(b), dk], writes=[dk])
                    if J > 0:
                        T.op("dve", lambda e, dst=dst, w0=w0, ch=ch: e.scalar_tensor_tensor(
                            out=dst[:, 0:2], in0=halo[:, 1 - par, ch, 0:2], scalar=w0, in1=dst[:, 0:2], op0=ALU.mult,
                            op1=ALU.add), reads=[("halo", 1 - par, ch), dk], writes=[dk])
                        T.op("dve", lambda e, dst=dst, w1=w1, ch=ch: e.scalar_tensor_tensor(
                            out=dst[:, 0:1], in0=halo[:, 1 - par, ch, 1:2], scalar=w1, in1=dst[:, 0:1], op0=ALU.mult,
                            op1=ALU.add), reads=[("halo", 1 - par, ch), dk], writes=[dk])
                    if J < 3:
                        T.op("act", lambda e, b=b, ch=ch: e.activation(out=halo[:, par, ch, 0:2], in_=bank(b)[:, 510:512],
                                                                       func=AF.Copy),
                             reads=[bkey(b)], writes=[("halo", par, ch)])
                T.op("act", lambda e, t=t: e.activation(out=ga[t][:, :], in_=ga[t][:, :], func=AF.Silu),
                     reads=[("ga", t)], writes=[("ga", t)])
                T.op(PENG, lambda e, t=t, g=g: e.tensor_tensor(out=actb[ab][:, g, :], in0=ga[t][:, :], in1=va[t][:, :],
                                                                op=ALU.mult),
                     reads=[("ga", t), ("va", t)], writes=[("actb", ab, g)])
        for j in range(4):
            i = 4 * J + j
            for half in range(2):
                b = next_bank([1, 5, 6, 7])
                for g in range(22):
                    T.op("pe", lambda e, g=g, j=j, b=b, half=half: e.matmul(
                        bank(b), lhsT=actb[ab][:, g, j * 128:(j + 1) * 128], rhs=wd[:, g, half * 512:(half + 1) * 512],
                        start=(g == 0), stop=(g == 21)), reads=[("actb", ab, g), ("wd", 0 if g < 11 else 1)], writes=[bkey(b)])
                T.op("dve", lambda e, i=i, b=b, half=half: e.tensor_tensor(
                    out=x_sb[:, i, half * 512:(half + 1) * 512], in0=bank(b), in1=x_sb[:, i, half * 512:(half + 1) * 512],
                    op=ALU.add), reads=[bkey(b), ("x", i)], writes=[("x", i)])

    rk = [0]
    for l in range(nlayers):
        emit_layer_setup(l)
        load_bc(gbc[:, :], rows_d[l, RW_GMIX:RW_GMIX + D], "gbc", "r3")
        def stage_a(J):
            T.mark("L%d A%d norm" % (l, J))
            emit_norm_to_hT(J)
            for ch_ in ORDER:
                if ch_ in "akvq":
                    emit_win(l, J, ch_)
                elif ch_ == "b":
                    emit_sgu(J)
                elif ch_ == "c":
                    emit_conv(J)

        def stage_b(J):
            T.mark("L%d B%d att" % (l, J))
            emit_attention(J)
            T.mark("L%d B%d wout" % (l, J))
            emit_wout(l, J)

        T.rank = rk[0]
        stage_a(0)
        for J in range(4):
            if J + 1 < 4:
                T.rank = rk[0] + (2 * J + 1) * RANKED
                stage_a(J + 1)
            T.rank = rk[0] + (2 * J + 1) * RANKED
            stage_b(J)
        rk[0] += 10
        T.rank = rk[0]
        T.barrier()
        load_bc(gbc[:, :], rows_d[l, RW_GFFN:RW_GFFN + D], "gbc", "r3")
        for hh in range(2):
            src = w_down_d[l].rearrange("(g p) c -> p g c", p=128)[:, 11 * hh:11 * hh + 11, :]
            T.dma("pool", lambda e, src=src, hh=hh: e.dma_start(out=wd[:, 11 * hh:11 * hh + 11, :], in_=src),
                  "wd%d" % hh, writes=[("wd", hh)])
        for J in range(4):
            T.rank = rk[0] + J * RANKED_F
            T.mark("L%d F%d" % (l, J))
            emit_ffn(l, J)
        rk[0] += 10
        T.rank = rk[0]
        if l + 1 < nlayers or os.environ.get("LAST_BARRIER"):
            T.barrier()
    load_bc(gbc[:, :], gfin_d, "gbc", "r3")
    for J in range(4):
        ss = misc[:, 0:4]
        T.op("dve", lambda e: e.memset(ss, 0.0), writes=["ss"])
        T.op("dve", lambda e: e.memset(misc[:, 12:16], 0.0), writes=["ss"])
        for j in range(4):
            i = 4 * J + j
            for hf in range(2):
                T.op("act", lambda e, i=i, j=j, hf=hf: e.activation(
                    out=junk[:, :], in_=x_sb[:, i, hf * 512:(hf + 1) * 512], func=AF.Square,
                    accum_out=misc[:, 12 * hf + j:12 * hf + j + 1]),
                    reads=[("x", i), "ss"], writes=["junk", ("ssj", j, hf)])
        T.op("dve", lambda e: e.tensor_tensor(out=misc[:, 4:8], in0=misc[:, 0:4], in1=misc[:, 12:16], op=ALU.add),
             reads=["ss"] + [("ssj", j, hf) for j in range(4) for hf in range(2)], writes=["rs"])
        emit_rstd(misc[:, 4:8], 4, 1.0 / D, "rs")
        for j in range(4):
            i = 4 * J + j
            T.op("dve", lambda e, i=i, j=j: e.scalar_tensor_tensor(
                out=x_sb[:, i, :], in0=x_sb[:, i, :], scalar=misc[:, 4 + j:5 + j], in1=gbc[:, :],
                op0=ALU.mult, op1=ALU.mult), reads=[("x", i), "rs", "gbc"], writes=[("x", i)])
        dst = out_d.rearrange("(i p) d -> p i d", p=128)[:, 4 * J:4 * J + 4, :]
        T.dma("sp", lambda e, J=J, dst=dst: e.dma_start(out=dst, in_=x_sb[:, 4 * J:4 * J + 4, :]), "o%d" % J,
              reads=[("x", 4 * J + j) for j in range(4)])
    T.wait_all_dma("sp")

    T.schedule()
    slots = sorted(T.dma_cnt.keys())
    sem_names = ["s_" + e for e in Tracker.ENG] + ["d_" + s for s in slots]
    import contextlib
    with contextlib.ExitStack() as es:
        sems = {}
        for e in Tracker.ENG:
            sems[e] = es.enter_context(nc.semaphore("s_" + e))
        for s in slots:
            sems[("dma", s)] = es.enter_context(nc.semaphore("d_" + s))
        block = es.enter_context(nc.Block())

        def replay(name, eng):
            for waits, fn, kind in T.ops[name]:
                for p, v in waits:
                    eng.wait_ge(sems[p], v)
                if fn is None:
                    continue
                ins = fn(eng)
                if kind[0] == "eng":
                    ins.then_inc(sems[name], 1)
                elif kind[0] == "dma":
                    ins.then_inc(sems[kind], 16)

        @block.tensor
        def _(e):
            replay("pe", e)

        @block.scalar
        def _(e):
            replay("act", e)

        @block.vector
        def _(e):
            replay("dve", e)

        @block.gpsimd
        def _(e):
            replay("pool", e)

        @block.sync
        def _(e):
            replay("sp", e)
    return nc, T


def host_pack(inputs):
    f = lambda a: np.ascontiguousarray(np.asarray(a, dtype=np.float32))
    L = 2
    pvec = np.zeros((L, 128, NPV), np.float32)
    rows = np.zeros((L, NRW), np.float32)
    conv_w = f(inputs["conv_w"])
    for l in range(L):
        for c in range(2):
            sl = slice(128 * c, 128 * (c + 1))
            pvec[l, :, PV_CW + 31 * c:PV_CW + 31 * (c + 1)] = conv_w[l][:, sl].T
            pvec[l, :, PV_CB + c] = f(inputs["conv_b"])[l, sl]
            pvec[l, :, PV_LG + c] = f(inputs["conv_ln_g"])[l, sl]
            pvec[l, :, PV_LB + c] = f(inputs["conv_ln_b"])[l, sl]
            pvec[l, :, PV_GA + c] = f(inputs["g_out"])[l, sl]
        pvec[l, :, PV_SB:PV_SB + 4] = f(inputs["sgu_b"])[l].T
        fw = f(inputs["ffn_conv_w"])[l]
        fb = f(inputs["ffn_conv_b"])[l]
        for c in range(44):
            sl = slice(128 * c, 128 * (c + 1))
            pvec[l, :, PV_FW + 3 * c:PV_FW + 3 * c + 3] = fw[:, sl].T
            pvec[l, :, PV_FB + c] = fb[sl]
        rows[l, RW_GMIX:RW_GMIX + D] = f(inputs["g_mix"])[l]
        rows[l, RW_GFFN:RW_GFFN + D] = f(inputs["g_ffn"])[l]
        rows[l, RW_GOUT:RW_GOUT + 768] = f(inputs["g_out"])[l, 256:]
        rows[l, RW_LNG:RW_LNG + 256] = f(inputs["sgu_ln_g"])[l]
        rows[l, RW_LNB:RW_LNB + 256] = f(inputs["sgu_ln_b"])[l]
    ident = np.eye(128, dtype=np.float32)
    kk, mm = np.meshgrid(np.arange(128), np.arange(128), indexing="ij")
    maskL = np.where(kk >= mm, NEG, 0.0).astype(np.float32)
    cbf = np.concatenate([ident, maskL], axis=1).astype(ml_dtypes.bfloat16)
    tril = (mm <= kk).astype(np.float32)
    cf32 = np.concatenate([np.ones((128, 128), np.float32), tril], axis=1)
    shared = {
        "w_in": f(inputs["w_in"]), "w_out": f(inputs["w_out"]), "w_up": f(inputs["w_up"]),
        "w_down": f(inputs["w_down"]), "sgu_w": f(inputs["sgu_w"]), "pvec": pvec, "rows": rows,
        "g_final": f(inputs["g_final"]), "cbf": cbf, "cf32": np.ascontiguousarray(cf32),
    }
    return shared


_CACHE = {}


def kernel(**inputs):
    x = np.asarray(inputs["x"], dtype=np.float32)
    shared = host_pack(inputs)
    if "nc" not in _CACHE:
        _CACHE["nc"] = build()[0]
    nc = _CACHE["nc"]
    in_maps = []
    for c in range(8):
        m = dict(shared)
        m["x"] = np.ascontiguousarray(x[c])
        in_maps.append(m)
    res = run_bass_kernel_spmd(nc, in_maps, core_ids=list(range(8)))
    out = np.stack([np.asarray(r["out"], dtype=np.float32) for r in res.results], axis=0)
    return out
```

```python
import os
import numpy as np
import ml_dtypes
import concourse.bass as bass
import concourse.mybir as mybir
from concourse.bass_utils import run_bass_kernel_spmd

F32 = mybir.dt.float32
BF16 = mybir.dt.bfloat16
AF = mybir.ActivationFunctionType
ALU = mybir.AluOpType

FULL_SYNC = True
SPARSE_INC = True
SGU_FIRST = True
ORDER = "akbcvq"
RANKED = 0
RANKED_F = 0
PENG = "pool"
S = 2048
D = 1024
NT = 16
DIN = 2560
DFF = 2816
EPS = 1e-6
NEG = -30.0

PV_CW = 0
PV_CB = 62
PV_LG = 64
PV_LB = 66
PV_GA = 68
PV_SB = 70
PV_FW = 74
PV_FB = 206
NPV = 256
RW_GMIX = 0
RW_GFFN = 1024
RW_GOUT = 2048
RW_LNG = 2816
RW_LNB = 3072
NRW = 3328


class _FakeIns:
    def then_inc(self, *a, **k):
        return self


class _FakeEng:
    def __init__(self):
        self.call = None

    def __getattr__(self, name):
        def f(*a, **k):
            self.call = (name, a, k)
            return _FakeIns()
        return f


def _fsz(ap):
    n = 1
    for d in ap.shape[1:]:
        n *= int(d)
    return n


def _is_psum(ap):
    return str(ap.space) == "PSUM"


_TABS = {"Sigmoid": "sig", "Silu": "silu", "Gelu": "gelu", "Sqrt": "sqrt", "Ln": "lnexp", "Exp": "lnexp"}


def _estimate(eng, fn):
    fe = _FakeEng()
    fn(fe)
    name, a, k = fe.call
    if name == "matmul":
        n = _fsz(k["rhs"])
        return max(0.056, 0.035 + n * 0.00042), 0.1, None
    if name == "transpose":
        return 0.135, 0.1, None
    if name == "dma_start":
        src = k["in_"]
        nbytes = 128 * _fsz(k["out"]) * 4
        return (1.0 if eng == "pool" else 0.15), 2.0 + nbytes / 330000.0, None
    if name == "nop":
        return 0.05, 0.0, None
    if name == "activation":
        n = _fsz(k["out"])
        fname = str(k["func"]).split(".")[-1]
        return 0.2 + 0.00078 * n, 0.05, _TABS.get(fname)
    out = k.get("out", a[0] if a else None)
    n = _fsz(out) if out is not None else 1
    if eng == "pool":
        return 0.15 + 0.00185 * n, 0.05, None
    if name == "tensor_tensor_scan":
        return 0.08 + 0.0022 * n, 0.05, None
    if name == "reciprocal":
        return 0.12 + 0.0022 * n, 0.05, None
    if name in ("memset",):
        return 0.05 + 0.0003 * n, 0.05, None
    if name in ("bn_stats", "bn_aggr"):
        return 0.3, 0.05, None
    if name == "tensor_copy":
        return 0.1 + 0.0008 * n, 0.05, None
    src = k.get("in0", k.get("in_", None))
    if src is not None and _is_psum(src):
        return 0.12 + 0.00146 * n, 0.05, None
    return 0.12 + 0.00129 * n, 0.05, None


class Op:
    __slots__ = ("id", "eng", "fn", "kind", "slot", "deps", "dur", "lat", "tab", "prio", "start", "fin", "seq",
                 "nsucc", "rank")


class Tracker:
    ENG = ("pe", "act", "dve", "pool", "sp")

    def __init__(self):
        self.oplist = []
        self.last_w = {}
        self.readers = {}
        self.last_dma = {}
        self.barrier_id = None
        self.since_barrier = []
        self.rank = 0

    def _mk(self, eng, fn, kind, slot, reads, writes):
        extra = [k for k in reads if isinstance(k, tuple) and k[0] in ("bank", "psT")]
        if extra:
            writes = list(writes) + [k for k in extra if k not in writes]
        deps = {}
        for k in reads:
            w = self.last_w.get(k)
            if w is not None:
                deps[w] = True
        for k in writes:
            w = self.last_w.get(k)
            if w is not None:
                deps.setdefault(w, False)
            for r in self.readers.get(k, ()):
                deps.setdefault(r, False)
        if self.barrier_id is not None:
            deps.setdefault(self.barrier_id, False)
        if kind == "dma" and slot in self.last_dma:
            deps[self.last_dma[slot]] = True
        op = Op()
        op.id = len(self.oplist)
        op.eng, op.fn, op.kind, op.slot, op.deps = eng, fn, kind, slot, deps
        op.rank = self.rank
        if fn is None:
            op.dur, op.lat, op.tab = 0.05, 0.0, None
        else:
            op.dur, op.lat, op.tab = _estimate(eng, fn)
        self.oplist.append(op)
        self.since_barrier.append(op.id)
        for k in writes:
            self.last_w[k] = op.id
            self.readers[k] = set()
        for k in reads:
            self.readers.setdefault(k, set()).add(op.id)
        if kind == "dma":
            self.last_dma[slot] = op.id
        return op

    def op(self, e, fn, reads=(), writes=()):
        self._mk(e, fn, "eng", None, reads, writes)

    def dma(self, q, fn, slot, reads=(), writes=()):
        self._mk(q, fn, "dma", slot, reads, writes)

    def barrier(self):
        op = Op()
        op.id = len(self.oplist)
        op.eng, op.fn, op.kind, op.slot = "virt", None, "virt", None
        op.deps = {i: True for i in self.since_barrier}
        op.rank = self.rank
        op.dur, op.lat, op.tab = 0.0, 0.0, None
        self.oplist.append(op)
        self.barrier_id = op.id
        self.since_barrier = [op.id]
        self.last_w = {}
        self.readers = {}

    def wait_all_dma(self, e):
        self.barrier()
        self.final_barrier = self.barrier_id

    def mark(self, name):
        self.marks = getattr(self, "marks", [])
        self.marks.append((name, len(self.oplist)))

    def schedule(self):
        import heapq
        stop = None
        if stop:
            k = int(stop)
            self.oplist = [o for o in self.oplist if o.id < k]
            fb = Op()
            fb.id = len(self.oplist)
            fb.eng, fb.fn, fb.kind, fb.slot = "virt", None, "virt", None
            fb.deps = {o.id: True for o in self.oplist}
            fb.dur, fb.lat, fb.tab, fb.rank = 0.0, 0.0, None, 10 ** 6
            self.oplist.append(fb)
            self.final_barrier = fb.id
        ops = self.oplist
        n = len(ops)
        succ = [[] for _ in range(n)]
        for o in ops:
            for d in o.deps:
                succ[d].append(o.id)
        for o in reversed(ops):
            best = 0.0
            for s_ in succ[o.id]:
                p = ops[s_].prio
                if p > best:
                    best = p
            o.prio = best + o.dur + o.lat
        for o in ops:
            o.prio = o.prio - 1.0e5 * o.rank
        if False:
            for o in ops:
                o.prio = -float(o.id)
        indeg = [len(o.deps) for o in ops]
        ready_t = [0.0] * n
        free = {e: 0.0 for e in self.ENG}
        cur_tab = [None]
        heapA = {e: [] for e in self.ENG}
        heapB = {e: [] for e in self.ENG}
        order = {e: [] for e in self.ENG}
        done = 0

        def release(i):
            nonlocal done
            stack = [i]
            while stack:
                j = stack.pop()
                oj = ops[j]
                if oj.eng == "virt":
                    oj.start = ready_t[j]
                    oj.fin = ready_t[j]
                    done += 1
                    for s2 in succ[j]:
                        if oj.fin > ready_t[s2]:
                            ready_t[s2] = oj.fin
                        indeg[s2] -= 1
                        if indeg[s2] == 0:
                            stack.append(s2)
                else:
                    heapq.heappush(heapA[oj.eng], (ready_t[j], -oj.prio, j))

        for o in ops:
            if indeg[o.id] == 0:
                release(o.id)
        while done < n:
            best_e, best_t = None, None
            for e in self.ENG:
                A, Bh = heapA[e], heapB[e]
                while A and A[0][0] <= free[e] + 1e-9:
                    rt, np_, i = heapq.heappop(A)
                    heapq.heappush(Bh, (np_, i))
                if Bh:
                    t = free[e]
                elif A:
                    t = A[0][0]
                else:
                    continue
                if best_t is None or t < best_t:
                    best_e, best_t = e, t
            e = best_e
            if heapB[e]:
                if e == "act" and cur_tab[0] is not None and len(heapB[e]) > 1:
                    cands = heapq.nsmallest(6, heapB[e])
                    pick = cands[0]
                    if ops[pick[1]].tab not in (None, cur_tab[0]):
                        for c in cands[1:]:
                            if ops[c[1]].tab in (None, cur_tab[0]) and -c[0] > -pick[0] - 12.0:
                                pick = c
                                break
                    heapB[e].remove(pick)
                    heapq.heapify(heapB[e])
                    np_, i = pick
                else:
                    np_, i = heapq.heappop(heapB[e])
            else:
                rt, np_, i = heapq.heappop(heapA[e])
            o = ops[i]
            st_ = max(free[e], ready_t[i])
            extra = 0.0
            if e == "act" and o.tab is not None and o.tab != cur_tab[0]:
                extra = 1.3
                cur_tab[0] = o.tab
            o.start = st_
            free[e] = st_ + extra + o.dur
            o.fin = free[e] + o.lat
            order[e].append(i)
            done += 1
            for s_ in succ[i]:
                if o.fin > ready_t[s_]:
                    ready_t[s_] = o.fin
                indeg[s_] -= 1
                if indeg[s_] == 0:
                    release(s_)
        self.order = order
        self.makespan = max(o.fin for o in ops)
        pos = {}
        for e in self.ENG:
            for k_, i in enumerate(order[e]):
                pos[i] = k_
        marked = set()
        for o in ops:
            if o.kind == "virt":
                last = {}
                for d in o.deps:
                    po = ops[d]
                    if po.kind == "eng":
                        if po.eng not in last or pos[d] > pos[last[po.eng]]:
                            last[po.eng] = d
                marked.update(last.values())
                continue
            last = {}
            for d, raw in o.deps.items():
                po = ops[d]
                if po.kind != "eng":
                    continue
                if po.eng == o.eng and (o.eng == "pe" or not (raw or FULL_SYNC)):
                    continue
                if po.eng not in last or pos[d] > pos[last[po.eng]]:
                    last[po.eng] = d
            marked.update(last.values())
        if not SPARSE_INC:
            marked = set(o.id for o in ops if o.kind == "eng")
        self.marked = marked
        self.dma_cnt = {}
        for e in self.ENG:
            c = 0
            for i in order[e]:
                o = ops[i]
                if o.kind == "eng":
                    if i in marked:
                        c += 1
                        o.seq = c
                    else:
                        o.seq = None
        for o in ops:
            if o.kind == "dma":
                self.dma_cnt[o.slot] = self.dma_cnt.get(o.slot, 0) + 1
                o.seq = 16 * self.dma_cnt[o.slot]
        self.ops = {e: [] for e in self.ENG}
        vneed = {}
        for o in ops:
            if o.kind == "virt":
                nd = {}
                for d in o.deps:
                    po = ops[d]
                    if po.kind == "virt":
                        for dom, v in vneed[d].items():
                            if v > nd.get(dom, 0):
                                nd[dom] = v
                        continue
                    dom = ("dma", po.slot) if po.kind == "dma" else po.eng
                    if po.seq is not None and po.seq > nd.get(dom, 0):
                        nd[dom] = po.seq
                vneed[o.id] = nd
        for e in self.ENG:
            known = {}
            for i in order[e]:
                o = ops[i]
                need = {}
                for d, raw in o.deps.items():
                    po = ops[d]
                    if po.kind == "virt":
                        for dom, v in vneed[d].items():
                            if dom != e and v > need.get(dom, 0):
                                need[dom] = v
                        continue
                    if po.kind == "dma":
                        dom = ("dma", po.slot)
                    else:
                        dom = po.eng
                        if dom == e and (e == "pe" or not raw) and not (FULL_SYNC and e != "pe"):
                            continue
                    if po.seq is not None and po.seq > need.get(dom, 0):
                        need[dom] = po.seq
                waits = []
                for dom, v in need.items():
                    if v > known.get(dom, 0):
                        known[dom] = v
                        waits.append((dom, v))
                if o.kind == "eng":
                    self.ops[e].append((waits, o.fn, ("eng", e) if o.id in marked else ("noinc", e)))
                else:
                    self.ops[e].append((waits, o.fn, ("dma", o.slot)))
        fb = getattr(self, "final_barrier", None)
        if fb is not None:
            self.ops["sp"].append(([(dom, v) for dom, v in vneed[fb].items()], None, None))


def build(nlayers=2, debug_out=None):
    nc = bass.Bass("TRN2", target_bir_lowering=False)
    L = 2
    x_d = nc.dram_tensor("x", [S, D], F32, kind="ExternalInput").ap()
    w_in_d = nc.dram_tensor("w_in", [L, D, DIN], F32, kind="ExternalInput").ap()
    w_out_d = nc.dram_tensor("w_out", [L, D, D], F32, kind="ExternalInput").ap()
    w_up_d = nc.dram_tensor("w_up", [L, D, 2 * DFF], F32, kind="ExternalInput").ap()
    w_down_d = nc.dram_tensor("w_down", [L, DFF, D], F32, kind="ExternalInput").ap()
    sgu_w_d = nc.dram_tensor("sgu_w", [L, 4, 128, 128], F32, kind="ExternalInput").ap()
    pvec_d = nc.dram_tensor("pvec", [L, 128, NPV], F32, kind="ExternalInput").ap()
    rows_d = nc.dram_tensor("rows", [L, NRW], F32, kind="ExternalInput").ap()
    gfin_d = nc.dram_tensor("g_final", [D], F32, kind="ExternalInput").ap()
    cb_d = nc.dram_tensor("cbf", [128, 256], BF16, kind="ExternalInput").ap()
    cf_d = nc.dram_tensor("cf32", [128, 256], F32, kind="ExternalInput").ap()
    out_d = nc.dram_tensor("out", [S, D], F32, kind="ExternalOutput").ap()

    off = [16512]

    def alloc(name, shape, dt, at=None):
        nbytes = int(np.prod(shape[1:])) * (4 if dt == F32 else 2)
        nbytes = (nbytes + 31) // 32 * 32
        if at is None:
            o = off[0]
            off[0] += nbytes
        else:
            o = at
        assert o + nbytes <= 229344, (name, o, nbytes)
        return nc.alloc_sbuf_tensor_at(name, list(shape), dt, offset=o), o + nbytes

    x_sb, _ = alloc("x_sb", [128, NT, D], F32)
    cbf, _ = alloc("cbf_sb", [128, 256], BF16)
    cf32, _ = alloc("cf32_sb", [128, 256], F32)
    pvec, _ = alloc("pvec_sb", [128, NPV], F32)
    wsT, _ = alloc("wsT", [128, 4, 128], BF16)
    misc, _ = alloc("misc", [128, 64], F32)
    gbc, _ = alloc("gbc", [128, D], F32)
    goutbc, _ = alloc("goutbc", [128, 768], F32)
    lngbc, _ = alloc("lngbc", [128, 256], F32)
    lnbbc, _ = alloc("lnbbc", [128, 256], F32)
    bsb, _ = alloc("bsb", [128, 256], F32)
    hb = [alloc("hb%d" % i, [128, D], BF16)[0] for i in range(2)]
    junk, _je = alloc("junk", [128, 512], BF16)
    ycn = junk
    wtmp, _ = alloc("wtmp", [128, 128], F32, at=_je - 1024)
    wtmpb, _ = alloc("wtmpb", [128, 128], BF16, at=_je - 512)
    hT, _ = alloc("hT", [128, 8, 512], BF16)
    wbA = []
    wbB = []
    for i in range(2):
        t, _ = alloc("wbA%d" % i, [128, 8, 512], BF16)
        wbA.append(t)
        wbB.append(alloc("wbB%d" % i, [128, 8, 2, 256], BF16, at=off[0] - 8192)[0])
    B = off[0]
    kT, _ = alloc("kT", [128, 4, S], BF16)
    v_sb, _ = alloc("v_sb", [128, NT, 512], BF16)
    qTs = [alloc("qT%d" % i, [128, 4, 512], BF16)[0] for i in range(2)]
    yTab = [alloc("yTab%d" % i, [128, 4, 512], BF16)[0] for i in range(2)]
    yTc, _ = alloc("yTc", [128, 4, 512], BF16)
    SC = off[0]
    hc, e1 = alloc("hc", [128, 2, 542], F32, at=SC)
    acc, e1 = alloc("acc", [128, 2, 512], F32, at=e1)
    yc, _ = alloc("yc", [128, 4, 512], F32, at=e1 - 4096)
    sq, e1 = alloc("sq", [128, 2, 512], F32, at=e1)
    st = []
    t, e1 = alloc("st0", [128, 512], F32, at=e1)
    st.append(t)
    ug, e2 = alloc("ug", [128, 2, 256], F32, at=e1)
    vg, e2 = alloc("vg", [128, 2, 256], F32, at=e2)
    stmp, e2 = alloc("stmp", [128, 256], F32, at=e2)
    vn, e2 = alloc("vn", [128, 256], BF16, at=e2)
    yb = ug
    ybn, e2 = alloc("ybn", [128, 256], BF16, at=e2)
    off[0] = e2
    GW = 2052
    G = [alloc("G%d" % i, [128, GW], F32)[0] for i in range(2)]
    A = [alloc("A%d" % i, [128, S], BF16)[0] for i in range(2)]
    AT, _ = alloc("AT", [128, S], BF16)
    mixer_end = off[0]
    off[0] = B
    wd, _ = alloc("wd", [128, 22, D], BF16)
    actb = [alloc("actb%d" % i, [128, 22, 512], BF16)[0] for i in range(2)]
    ga = [alloc("ga%d" % i, [128, 512], F32)[0] for i in range(2)]
    va = [alloc("va%d" % i, [128, 512], F32)[0] for i in range(2)]
    halo, _ = alloc("halo", [128, 2, 44, 4], F32)
    ffn_end = off[0]

    identb = cbf[:, 0:128]
    maskL = cbf[:, 128:256]
    ones32 = cf32[:, 0:128]
    tril = cf32[:, 128:256]

    psT = nc.alloc_psum_tensor("psT", [128, 2048], BF16)
    psTf = psT.bitcast(F32)
    psM = nc.alloc_psum_tensor("psM", [128, 1024], F32)
    psZ = nc.alloc_psum_tensor("psZ", [128, 1024], F32)
    psW = nc.alloc_psum_tensor("psW", [128, 1024], F32)

    def bank(b):
        if b < 2:
            return psM[:, b * 512:(b + 1) * 512]
        if b < 4:
            return psZ[:, (b - 2) * 512:(b - 1) * 512]
        if b < 6:
            return psW[:, (b - 4) * 512:(b - 3) * 512]
        return psTf[:, (b - 6) * 512:(b - 5) * 512]

    def bkey(b):
        if b >= 6:
            return ("psT", b - 6)
        return ("bank", b)

    T = Tracker()
    rot = {"i": 0}

    def next_bank(pool):
        rot["i"] += 1
        return pool[rot["i"] % len(pool)]

    def pv(col):
        return pvec[:, col:col + 1]

    wjobs = []
    wstate = {"issued": 0}

    def issue_job(n):
        kind, l, a = wjobs[n]
        s = n % 2
        if kind == "in":
            src = w_in_d[l].rearrange("(k p) c -> p k c", p=128)[:, :, a:a + 512]
            dst = wbA[s]
            T.dma("pool", lambda e, dst=dst, src=src: e.dma_start(out=dst[:, :, :], in_=src),
                  "wb%d" % s, writes=[("wb", s), ("wbv", s)])
        elif kind == "out":
            src = w_out_d[l].rearrange("(k p) c -> p k c", p=128)[:, :, a:a + 512]
            dst = wbA[s]
            T.dma("pool", lambda e, dst=dst, src=src: e.dma_start(out=dst[:, :, :], in_=src),
                  "wb%d" % s, writes=[("wb", s), ("wbv", s)])
        elif kind == "up":
            for which in range(2):
                src = w_up_d[l].rearrange("(k p) c -> p k c", p=128)[:, :, which * DFF + a:which * DFF + a + 256]
                dst = wbB[s]
                T.dma("pool", lambda e, dst=dst, src=src, which=which: e.dma_start(out=dst[:, :, which, :], in_=src),
                      ("wb%d" if which == 0 else "wv%d") % s, writes=[("wb" if which == 0 else "wbv", s)])

    def get_w(n):
        while wstate["issued"] <= min(n + 1, len(wjobs) - 1):
            issue_job(wstate["issued"])
            wstate["issued"] += 1
        return n % 2

    for l in range(nlayers):
        def _ja():
            for ch_ in ORDER:
                if ch_ != "c":
                    wjobs.append(("in", l, {"a": 0, "k": 1536, "v": 2048, "q": 1024, "b": 512}[ch_]))

        def _jb():
            for c0 in (0, 512):
                wjobs.append(("out", l, c0))

        _ja()
        for J in range(4):
            if J + 1 < 4:
                _ja()
            _jb()
        for J in range(4):
            for gg in range(11):
                wjobs.append(("up", l, gg * 256))
    wj = {"n": 0}

    def take_w():
        n = wj["n"]
        wj["n"] += 1
        return get_w(n)

    for q in range(4):
        src = x_d.rearrange("(i p) d -> p i d", p=128)[:, 4 * q:4 * q + 4, :]
        T.dma("sp", lambda e, q=q, src=src: e.dma_start(out=x_sb[:, 4 * q:4 * q + 4, :], in_=src),
              "x%d" % q, writes=[("x", 4 * q + j) for j in range(4)])
    T.dma("sp", lambda e: e.dma_start(out=cbf[:, :], in_=cb_d), "c0", writes=["cbf"])
    T.dma("sp", lambda e: e.dma_start(out=cf32[:, :], in_=cf_d), "c1", writes=["cf32"])

    def load_bc(dst, src_row, key, slot):
        T.dma("sp", lambda e, dst=dst, src_row=src_row: e.dma_start(out=dst, in_=src_row.partition_broadcast(128)),
              slot, writes=[key])

    def emit_rstd(ssap, n, scale, key):
        T.op("dve", lambda e: e.tensor_scalar(out=ssap, in0=ssap, scalar1=scale, scalar2=EPS, op0=ALU.mult, op1=ALU.add),
             reads=[key], writes=[key])
        T.op("act", lambda e: e.activation(out=ssap, in_=ssap, func=AF.Sqrt), reads=[key], writes=[key])
        T.op("dve", lambda e: e.reciprocal(out=ssap, in_=ssap), reads=[key], writes=[key])

    def emit_norm_to_hT(J):
        ss = misc[:, 0:4]
        T.op("dve", lambda e: e.memset(ss, 0.0), writes=["ss"])
        T.op("dve", lambda e: e.memset(misc[:, 12:16], 0.0), writes=["ss"])
        for j in range(4):
            i = 4 * J + j
            for hf in range(2):
                T.op("act", lambda e, i=i, j=j, hf=hf: e.activation(
                    out=junk[:, :], in_=x_sb[:, i, hf * 512:(hf + 1) * 512], func=AF.Square,
                    accum_out=misc[:, 12 * hf + j:12 * hf + j + 1]),
                    reads=[("x", i), "ss"], writes=["junk", ("ssj", j, hf)])
        T.op("dve", lambda e: e.tensor_tensor(out=misc[:, 4:8], in0=misc[:, 0:4], in1=misc[:, 12:16], op=ALU.add),
             reads=["ss"] + [("ssj", j, hf) for j in range(4) for hf in range(2)], writes=["rs"])
        emit_rstd(misc[:, 4:8], 4, 1.0 / D, "rs")
        for j in range(4):
            i = 4 * J + j
            b = j % 2
            T.op("dve", lambda e, i=i, j=j, b=b: e.scalar_tensor_tensor(
                out=hb[b][:, :], in0=x_sb[:, i, :], scalar=misc[:, 4 + j:5 + j], in1=gbc[:, :],
                op0=ALU.mult, op1=ALU.mult), reads=[("x", i), "rs", "gbc"], writes=[("hb", b)])
            for k in range(8):
                T.op("pe", lambda e, k=k, b=b: e.transpose(psT[:, b * 1024 + k * 128: b * 1024 + (k + 1) * 128],
                                                          hb[b][:, k * 128:(k + 1) * 128], identb),
                     reads=[("hb", b), "cbf"], writes=[("psT", b)])
            T.op("act", lambda e, j=j, b=b: e.activation(
                out=hT[:, :, j * 128:(j + 1) * 128],
                in_=psT[:, b * 1024:(b + 1) * 1024].rearrange("p (k t) -> p k t", k=8), func=AF.Copy),
                reads=[("psT", b)], writes=[("hT", j)])

    hT_keys = [("hT", j) for j in range(4)]
    WIDE = [0, 4, 5]
    GEN = [0, 4, 5]

    def mm_group_fm(s, fc, b, wview):
        for k in range(8):
            T.op("pe", lambda e, k=k, fc=fc, b=b, s=s: e.matmul(bank(b), lhsT=wview[s][:, k, fc * 128:(fc + 1) * 128],
                                                               rhs=hT[:, k, :], start=(k == 0), stop=(k == 7)),
                 reads=[("wb", s), ("wbv", s)] + hT_keys, writes=[bkey(b)])

    def mm_group_tm(s, j, b):
        for k in range(8):
            T.op("pe", lambda e, k=k, j=j, b=b, s=s: e.matmul(bank(b), lhsT=hT[:, k, j * 128:(j + 1) * 128],
                                                             rhs=wbA[s][:, k, :], start=(k == 0), stop=(k == 7)),
                 reads=[("wb", s), ("wbv", s), ("hT", j)], writes=[bkey(b)])

    def emit_layer_setup(l):
        T.dma("sp", lambda e, l=l: e.dma_start(out=pvec[:, :], in_=pvec_d[l]), "pv", writes=["pvec"])
        load_bc(goutbc[:, :], rows_d[l, RW_GOUT:RW_GOUT + 768], "goutbc", "r0")
        load_bc(lngbc[:, :], rows_d[l, RW_LNG:RW_LNG + 256], "lngbc", "r1")
        load_bc(lnbbc[:, :], rows_d[l, RW_LNB:RW_LNB + 256], "lnbbc", "r2")
        T.op("dve", lambda e: e.memset(bsb[:, :], 0.0), writes=["bsb"])
        for h in range(4):
            T.op("dve", lambda e, h=h: e.tensor_scalar_add(out=bsb[:, 64 * h:64 * h + 64], in0=bsb[:, 64 * h:64 * h + 64],
                                                           scalar1=pv(PV_SB + h)), reads=["pvec", "bsb"], writes=["bsb"])
        for h in range(4):
            T.dma("sp", lambda e, l=l, h=h: e.dma_start(out=wtmp[:, :], in_=sgu_w_d[l, h]), "ws", writes=["wtmp", "junk"])
            T.op("dve", lambda e: e.tensor_tensor(out=wtmpb[:, :], in0=wtmp[:, :], in1=tril, op=ALU.mult),
                 reads=["wtmp", "cf32", "junk"], writes=["wtmpb", "junk"])
            T.op("pe", lambda e: e.transpose(psT[:, 0:128], wtmpb[:, :], identb), reads=["wtmpb", "cbf", "junk"],
                 writes=[("psT", 0)])
            T.op("act", lambda e, h=h: e.activation(out=wsT[:, h, :], in_=psT[:, 0:128], func=AF.Copy),
                 reads=[("psT", 0)], writes=["wsT"])
        for b in range(2):
            T.op("dve", lambda e, b=b: e.memset(G[b][:, 2048:2049], 1.0), writes=[("G", b)])

    def emit_win(l, J, letters="akvq"):
        evq = {"i": 0}

        def cp(out, in_, reads, writes, scale=None):
            evq["i"] += 1
            if scale is not None or evq["i"] % 2 == 0:
                if scale is None:
                    T.op("act", lambda e: e.activation(out=out, in_=in_, func=AF.Copy), reads=reads, writes=writes)
                else:
                    T.op("act", lambda e: e.activation(out=out, in_=in_, func=AF.Copy, scale=scale), reads=reads,
                         writes=writes)
            else:
                T.op("dve", lambda e: e.tensor_copy(out=out, in_=in_), reads=reads, writes=writes)

        def grp_a():
            s = take_w()
            if J == 0:
                T.op("dve", lambda e: e.memset(hc[:, :, 0:30], 0.0), writes=["hc"])
            else:
                T.op("dve", lambda e: e.tensor_copy(out=hc[:, :, 0:30], in_=hc[:, :, 512:542]), reads=["hc"],
                     writes=["hc"])
            for c in range(2):
                b = next_bank(WIDE)
                mm_group_fm(s, 2 + c, b, wbA)
                T.op("act", lambda e, c=c, b=b: e.activation(out=sq[:, c, :], in_=bank(b), func=AF.Sigmoid),
                     reads=[bkey(b)], writes=[("sq", c)])
            for c in range(2):
                b = next_bank(WIDE)
                mm_group_fm(s, c, b, wbA)
                T.op("dve", lambda e, c=c, b=b: e.tensor_tensor(out=hc[:, c, 30:542], in0=bank(b), in1=sq[:, c, :],
                                                                op=ALU.mult),
                     reads=[bkey(b), ("sq", c), "hc"], writes=["hc"])

        def grp_k():
            s = take_w()
            for fc in range(4):
                b = next_bank(WIDE)
                mm_group_fm(s, fc, b, wbA)
                cp(kT[:, fc, J * 512:(J + 1) * 512], bank(b), [bkey(b)], [("kT", J)])

        def grp_v():
            s = take_w()
            for j in range(4):
                b = next_bank(WIDE)
                mm_group_tm(s, j, b)
                cp(v_sb[:, 4 * J + j, :], bank(b), [bkey(b)], [("v", 4 * J + j)])

        def grp_q():
            s = take_w()
            for fc in range(4):
                b = next_bank(WIDE)
                mm_group_fm(s, fc, b, wbA)
                cp(qTs[J % 2][:, fc, :], bank(b), [bkey(b)], [("qT", J % 2)], scale=0.125)

        for ch_ in letters:
            {"a": grp_a, "k": grp_k, "v": grp_v, "q": grp_q}[ch_]()

    def emit_conv(J):
        for k in range(31):
            for c in range(2):
                if k == 0:
                    T.op("dve", lambda e, c=c: e.tensor_scalar(out=acc[:, c, :], in0=hc[:, c, 0:512],
                                                               scalar1=pv(PV_CW + c * 31), scalar2=pv(PV_CB + c),
                                                               op0=ALU.mult, op1=ALU.add),
                         reads=["hc", "pvec"], writes=[("acc", c)])
                else:
                    T.op("dve", lambda e, c=c, k=k: e.scalar_tensor_tensor(
                        out=acc[:, c, :], in0=hc[:, c, k:k + 512], scalar=pv(PV_CW + c * 31 + k), in1=acc[:, c, :],
                        op0=ALU.mult, op1=ALU.add), reads=["hc", ("acc", c)], writes=[("acc", c)])
        T.op("act", lambda e: e.activation(out=sq[:, :, :], in_=acc[:, :, :], func=AF.Square),
             reads=[("acc", 0), ("acc", 1)], writes=[("sq", 0), ("sq", 1)])
        b1 = next_bank(GEN)
        for c in range(2):
            T.op("pe", lambda e, c=c, b1=b1: e.matmul(bank(b1), lhsT=ones32, rhs=acc[:, c, :], start=(c == 0),
                                                      stop=(c == 1)), reads=[("acc", c), "cf32"], writes=[bkey(b1)])
        T.op("act", lambda e, b1=b1: e.activation(out=st[0][:, :], in_=bank(b1), func=AF.Copy, scale=1.0 / 256),
             reads=[bkey(b1)], writes=["st0"])
        b2 = next_bank(GEN)
        for c in range(2):
            T.op("pe", lambda e, c=c, b2=b2: e.matmul(bank(b2), lhsT=ones32, rhs=sq[:, c, :], start=(c == 0),
                                                      stop=(c == 1)), reads=[("sq", c), "cf32"], writes=[bkey(b2)])
        T.op("dve", lambda e: e.tensor_tensor(out=sq[:, 1, :], in0=st[0][:, :], in1=st[0][:, :], op=ALU.mult),
             reads=["st0"], writes=[("sq", 1)])
        T.op("dve", lambda e, b2=b2: e.scalar_tensor_tensor(out=sq[:, 1, :], in0=bank(b2), scalar=1.0 / 256,
                                                            in1=sq[:, 1, :], op0=ALU.mult, op1=ALU.subtract),
             reads=[bkey(b2), ("sq", 1)], writes=[("sq", 1)])
        T.op("dve", lambda e: e.tensor_scalar_add(out=sq[:, 1, :], in0=sq[:, 1, :], scalar1=EPS), reads=[("sq", 1)],
             writes=[("sq", 1)])
        T.op("act", lambda e: e.activation(out=sq[:, 1, :], in_=sq[:, 1, :], func=AF.Ln), reads=[("sq", 1)],
             writes=[("sq", 1)])
        T.op("act", lambda e: e.activation(out=sq[:, 1, :], in_=sq[:, 1, :], func=AF.Exp, scale=-0.5), reads=[("sq", 1)],
             writes=[("sq", 1)])
        for c in range(2):
            T.op("dve", lambda e, c=c: e.tensor_tensor(out=acc[:, c, :], in0=acc[:, c, :], in1=st[0][:, :],
                                                       op=ALU.subtract), reads=[("acc", c), "st0"], writes=[("acc", c)])
        for c in range(2):
            T.op("dve", lambda e, c=c: e.tensor_tensor(out=acc[:, c, :], in0=acc[:, c, :], in1=sq[:, 1, :],
                                                       op=ALU.mult), reads=[("acc", c), ("sq", 1)], writes=[("acc", c)])
        for c in range(2):
            T.op("act", lambda e, c=c: e.activation(out=acc[:, c, :], in_=acc[:, c, :], func=AF.Silu,
                                                    scale=pv(PV_LG + c), bias=pv(PV_LB + c)),
                 reads=[("acc", c), "pvec"], writes=[("acc", c)])
        T.op("act", lambda e: e.activation(out=sq[:, :, :], in_=acc[:, :, :], func=AF.Square),
             reads=[("acc", 0), ("acc", 1)], writes=[("sq", 0), ("sq", 1)])
        b3 = next_bank(GEN)
        for c in range(2):
            T.op("pe", lambda e, c=c, b3=b3: e.matmul(bank(b3), lhsT=ones32, rhs=sq[:, c, :], start=(c == 0),
                                                      stop=(c == 1)), reads=[("sq", c), "cf32"], writes=[bkey(b3)])
        T.op("dve", lambda e, b3=b3: e.tensor_scalar(out=st[0][:, :], in0=bank(b3), scalar1=1.0 / 256, scalar2=EPS,
                                                     op0=ALU.mult, op1=ALU.add), reads=[bkey(b3)], writes=["st0"])
        T.op("act", lambda e: e.activation(out=st[0][:, :], in_=st[0][:, :], func=AF.Ln), reads=["st0"],
             writes=["st0"])
        T.op("act", lambda e: e.activation(out=st[0][:, :], in_=st[0][:, :], func=AF.Exp, scale=-0.5), reads=["st0"],
             writes=["st0"])
        for c in range(2):
            T.op("dve", lambda e, c=c: e.scalar_tensor_tensor(out=yTab[J % 2][:, c, :], in0=acc[:, c, :], scalar=pv(PV_GA + c),
                                                              in1=st[0][:, :], op0=ALU.mult, op1=ALU.mult),
                 reads=[("acc", c), "st0", "pvec"], writes=[("yT", J % 2, c)])

    def emit_sgu(J):
        s = take_w()
        for half in range(2):
            tiles = [2 * half, 2 * half + 1]
            for j in tiles:
                jl = j - 2 * half
                b = next_bank(GEN)
                mm_group_tm(s, j, b)
                T.op("act", lambda e, jl=jl, b=b: e.activation(out=ug[:, jl, :], in_=bank(b)[:, 0:256], func=AF.Gelu),
                     reads=[bkey(b)], writes=[("ug", jl)])
                T.op("act", lambda e, jl=jl, b=b: e.activation(out=vg[:, jl, :], in_=bank(b)[:, 256:512], func=AF.Gelu),
                     reads=[bkey(b)], writes=[("vg", jl)])
            emit_sgu_half(J, half)

    def emit_sgu_half(J, half):
        j0 = 2 * half
        rs4 = misc[:, 48 + j0:50 + j0]
        ssb = misc[:, 52 + j0:54 + j0]
        rsb = misc[:, 56 + j0:58 + j0]
        for jl in range(2):
            j = j0 + jl
            T.op("dve", lambda e, j=j, jl=jl: e.bn_stats(out=misc[:, 16 + 6 * j:22 + 6 * j], in_=vg[:, jl, :]),
                 reads=[("vg", jl)], writes=[("bst", j)])
            T.op("dve", lambda e, j=j: e.bn_aggr(out=misc[:, 40 + 2 * j:42 + 2 * j], in_=misc[:, 16 + 6 * j:22 + 6 * j]),
                 reads=[("bst", j)], writes=[("mv", j)])
        T.op("dve", lambda e: e.tensor_scalar_add(
            out=rs4, in0=misc[:, 40 + 2 * j0:44 + 2 * j0].rearrange("p (j two) -> p j two", two=2)[:, :, 1], scalar1=EPS),
            reads=[("mv", j0), ("mv", j0 + 1)], writes=[("rs4", half)])
        T.op("act", lambda e: e.activation(out=rs4, in_=rs4, func=AF.Sqrt), reads=[("rs4", half)], writes=[("rs4", half)])
        T.op("dve", lambda e: e.reciprocal(out=rs4, in_=rs4), reads=[("rs4", half)], writes=[("rs4", half)])
        T.op("dve", lambda e: e.memset(ssb, 0.0), writes=[("ssb", half)])
        for jl in range(2):
            j = j0 + jl
            T.op("dve", lambda e, j=j, jl=jl: e.tensor_scalar(out=stmp[:, :], in0=vg[:, jl, :],
                                                             scalar1=misc[:, 40 + 2 * j:41 + 2 * j],
                                                             scalar2=misc[:, 48 + j:49 + j], op0=ALU.subtract, op1=ALU.mult),
                 reads=[("vg", jl), ("mv", j), ("rs4", half)], writes=["stmp"])
            T.op("dve", lambda e: e.tensor_tensor(out=stmp[:, :], in0=stmp[:, :], in1=lngbc[:, :], op=ALU.mult),
                 reads=["stmp", "lngbc"], writes=["stmp"])
            T.op("dve", lambda e: e.tensor_tensor(out=vn[:, :], in0=stmp[:, :], in1=lnbbc[:, :], op=ALU.add),
                 reads=["stmp", "lnbbc"], writes=["vn"])
            b = next_bank(GEN)
            for h in range(4):
                T.op("pe", lambda e, h=h, b=b: e.matmul(bank(b)[:, 64 * h:64 * h + 64], lhsT=wsT[:, h, :],
                                                        rhs=vn[:, 64 * h:64 * h + 64], start=True, stop=True),
                     reads=["vn", "wsT"], writes=[bkey(b)])
            T.op("dve", lambda e, b=b: e.tensor_tensor(out=stmp[:, :], in0=bank(b)[:, 0:256], in1=bsb[:, :], op=ALU.add),
                 reads=[bkey(b), "bsb"], writes=["stmp"])
            T.op("dve", lambda e, jl=jl: e.tensor_tensor(out=yb[:, jl, :], in0=stmp[:, :], in1=ug[:, jl, :], op=ALU.mult),
                 reads=["stmp", ("ug", jl)], writes=[("ug", jl)])
            T.op("act", lambda e, j=j, jl=jl: e.activation(out=junk[:, 0:256], in_=yb[:, jl, :], func=AF.Square,
                                                           accum_out=misc[:, 52 + j:53 + j]),
                 reads=[("ug", jl), ("ssb", half)], writes=["junk", ("ssbj", j)])
        T.op("dve", lambda e: e.tensor_copy(out=rsb, in_=ssb), reads=[("ssb", half), ("ssbj", j0), ("ssbj", j0 + 1)],
             writes=[("rsb", half)])
        emit_rstd(rsb, 2, 1.0 / 256, ("rsb", half))
        for jl in range(2):
            j = j0 + jl
            T.op("dve", lambda e, j=j, jl=jl: e.scalar_tensor_tensor(out=ybn[:, :], in0=yb[:, jl, :],
                                                                    scalar=misc[:, 56 + j:57 + j],
                                                                    in1=goutbc[:, 0:256], op0=ALU.mult, op1=ALU.mult),
                 reads=[("ug", jl), ("rsb", half), "goutbc"], writes=["ybn"])
            for k in range(2):
                T.op("pe", lambda e, k=k: e.transpose(psT[:, 1024 + k * 128:1024 + (k + 1) * 128],
                                                      ybn[:, k * 128:(k + 1) * 128], identb),
                     reads=["ybn", "cbf"], writes=[("psT", 1)])
            T.op("act", lambda e, j=j: e.activation(out=yTab[J % 2][:, 2:4, j * 128:(j + 1) * 128],
                                                    in_=psT[:, 1024:1280].rearrange("p (k t) -> p k t", k=2),
                                                    func=AF.Copy), reads=[("psT", 1)], writes=[("yT", J % 2, 2), ("yT", J % 2, 3)])

    def emit_attention(J):
        units = [(j, h) for j in range(4) for h in range(8)]
        zkeys = [("bank", 2), ("bank", 3), ("bank", 4), ("bank", 5)]
        ssc = misc[:, 60:64]

        def st_z(ui):
            j, h = units[ui]
            i = 4 * J + j
            n = 128 * (i + 1)
            fc, po = h // 2, 64 * (h % 2)
            nb = (n + 511) // 512
            g = ui % 2
            for kb in range(nb):
                c0 = kb * 512
                w = min(512, n - c0)
                last = kb == nb - 1
                zb = kb % 2
                zk = ("bank", 2 + zb)
                T.op("pe", lambda e, c0=c0, w=w, last=last, fc=fc, po=po, j=j, zb=zb: e.matmul(
                    psZ[:, zb * 512:zb * 512 + w], lhsT=qTs[J % 2][po:po + 64, fc, j * 128:(j + 1) * 128],
                    rhs=kT[po:po + 64, fc, c0:c0 + w], start=True, stop=not last),
                    reads=[("qT", J % 2)] + [("kT", jj) for jj in range(J + 1)], writes=[zk])
                if last:
                    T.op("pe", lambda e, w=w, zb=zb: e.matmul(psZ[:, zb * 512 + w - 128:zb * 512 + w], lhsT=maskL,
                                                             rhs=identb, start=False, stop=True),
                         reads=["cbf"], writes=[zk])
                T.op("act", lambda e, n=n, g=g, c0=c0, w=w, zb=zb: e.activation(
                    out=G[g][:, 2048 - n + c0:2048 - n + c0 + w], in_=psZ[:, zb * 512:zb * 512 + w], func=AF.Sigmoid,
                    scale=-1.0), reads=[zk], writes=[("G", g)])

        def st_sig(ui):
            pass

        def st_scan(ui):
            j, h = units[ui]
            n = 128 * (4 * J + j + 1)
            g = ui % 2
            rev = bass.AP(G[g], 2047, [[GW, 128], [-1, n]])
            T.op("dve", lambda e, rev=rev: e.tensor_tensor_scan(out=rev, data0=rev, data1=rev, initial=1.0,
                                                                op0=ALU.mult, op1=ALU.bypass),
                 reads=[("G", g)], writes=[("G", g)])
            T.op("dve", lambda e, n=n, g=g: e.tensor_tensor(out=A[g][:, 0:n], in0=G[g][:, 2049 - n:2049],
                                                            in1=G[g][:, 2048 - n:2048], op=ALU.subtract),
                 reads=[("G", g)], writes=[("A", g)])

        def st_tr(ui):
            j, h = units[ui]
            i = 4 * J + j
            g = ui % 2
            for c in range(i + 1):
                T.op("pe", lambda e, c=c, g=g: e.transpose(psT[:, c * 128:(c + 1) * 128], A[g][:, c * 128:(c + 1) * 128],
                                                           identb),
                     reads=[("A", g), "cbf"], writes=[("psT", 1)] if c >= 8 else [("psT", 0)])

        def st_cp(ui):
            j, h = units[ui]
            n = 128 * (4 * J + j + 1)
            T.op("act", lambda e, n=n: e.activation(out=AT[:, 0:n], in_=psT[:, 0:n], func=AF.Copy),
                 reads=[("psT", 0), ("psT", 1)] if n > 1024 else [("psT", 0)], writes=["AT"])

        def st_av(ui):
            j, h = units[ui]
            i = 4 * J + j
            for c in range(i + 1):
                T.op("pe", lambda e, c=c, h=h, i=i: e.matmul(bank(1)[:, 64 * h:64 * h + 64],
                                                             lhsT=AT[:, c * 128:(c + 1) * 128],
                                                             rhs=v_sb[:, c, 64 * h:64 * h + 64], start=(c == 0),
                                                             stop=(c == i)),
                     reads=["AT", ("v", c)], writes=[("bank", 1)])
            if h == 7:
                T.op("act", lambda e, j=j: e.activation(out=yc[:, j, :], in_=bank(1), func=AF.Copy),
                     reads=[("bank", 1)], writes=[("yc", j), ("acc", 0), ("acc", 1), ("sq", 0), ("sq", 1)])
                T.op("act", lambda e, j=j: e.activation(out=junk[:, 0:512], in_=yc[:, j, :], func=AF.Square,
                                                        accum_out=misc[:, 60 + j:61 + j]),
                     reads=[("yc", j), "ssc", ("acc", 0), ("acc", 1), ("sq", 0), ("sq", 1)], writes=["junk", ("sscj", j)])

        T.op("dve", lambda e: e.memset(ssc, 0.0), writes=["ssc"])
        for ui in range(len(units)):
            for st_fn in (st_z, st_scan, st_tr, st_cp, st_av):
                st_fn(ui)
        T.op("dve", lambda e: e.tensor_copy(out=misc[:, 8:12], in_=ssc), reads=["ssc"] + [("sscj", j) for j in range(4)],
             writes=["rsc"])
        emit_rstd(misc[:, 8:12], 4, 1.0 / 512, "rsc")
        for j in range(4):
            T.op("dve", lambda e, j=j: e.scalar_tensor_tensor(out=ycn[:, :], in0=yc[:, j, :], scalar=misc[:, 8 + j:9 + j],
                                                              in1=goutbc[:, 256:768], op0=ALU.mult, op1=ALU.mult),
                 reads=[("yc", j), "rsc", "goutbc", ("acc", 0), ("acc", 1), ("sq", 0), ("sq", 1)], writes=["junk"])
            for k in range(4):
                T.op("pe", lambda e, k=k: e.transpose(psT[:, 1024 + k * 128:1024 + (k + 1) * 128],
                                                      ycn[:, k * 128:(k + 1) * 128], identb),
                     reads=["junk", "cbf"], writes=[("psT", 1)])
            T.op("act", lambda e, j=j: e.activation(out=yTc[:, 0:4, j * 128:(j + 1) * 128],
                                                    in_=psT[:, 1024:1536].rearrange("p (k t) -> p k t", k=4),
                                                    func=AF.Copy), reads=[("psT", 1)],
                 writes=[("yTc", 0)])

    def emit_wout(l, J):
        for gidx in range(2):
            s = take_w()
            for j in range(4):
                i = 4 * J + j
                b = next_bank(GEN)
                for k in range(8):
                    T.op("pe", lambda e, k=k, j=j, b=b, s=s: e.matmul(bank(b), lhsT=(yTab[J % 2][:, k, j * 128:(j + 1) * 128] if k < 4 else yTc[:, k - 4, j * 128:(j + 1) * 128]),
                                                                     rhs=wbA[s][:, k, :], start=(k == 0), stop=(k == 7)),
                         reads=[("wb", s), ("wbv", s), ("yTc", 0)] + [("yT", J % 2, kk) for kk in range(4)], writes=[bkey(b)])
                T.op("dve", lambda e, i=i, b=b, gidx=gidx: e.tensor_tensor(
                    out=x_sb[:, i, gidx * 512:(gidx + 1) * 512], in0=bank(b), in1=x_sb[:, i, gidx * 512:(gidx + 1) * 512],
                    op=ALU.add), reads=[bkey(b), ("x", i)], writes=[("x", i)])

    def emit_ffn(l, J):
        emit_norm_to_hT(J)
        ab = J % 2
        par = J % 2
        UPB = [[2, 3], [4, 0]]
        ui = 0
        for gg in range(11):
            s = take_w()
            for pi in range(2):
                g = 2 * gg + pi
                bs = UPB[ui % 2]
                ui += 1
                t = ui % 2
                for which in range(2):
                    b = bs[which]
                    for k in range(8):
                        T.op("pe", lambda e, k=k, b=b, s=s, which=which, pi=pi: e.matmul(
                            bank(b), lhsT=wbB[s][:, k, which, pi * 128:(pi + 1) * 128], rhs=hT[:, k, :],
                            start=(k == 0), stop=(k == 7)), reads=[("wb" if which == 0 else "wbv", s)] + hT_keys,
                            writes=[bkey(b)])
                for which in range(2):
                    b = bs[which]
                    ch = g + 22 * which
                    dst = ga[t] if which == 0 else va[t]
                    dk = ("ga", t) if which == 0 else ("va", t)
                    w0, w1, w2 = pv(PV_FW + 3 * ch), pv(PV_FW + 3 * ch + 1), pv(PV_FW + 3 * ch + 2)
                    T.op("act", lambda e, b=b, dst=dst, w2=w2, ch=ch: e.activation(
                        out=dst[:, :], in_=bank(b), func=AF.Identity, scale=w2, bias=pv(PV_FB + ch)),
                        reads=[bkey(b), "pvec"], writes=[dk])
                    T.op("dve", lambda e, b=b, dst=dst, w1=w1: e.scalar_tensor_tensor(
                        out=dst[:, 1:512], in0=bank(b)[:, 0:511], scalar=w1, in1=dst[:, 1:512], op0=ALU.mult,
                        op1=ALU.add), reads=[bkey(b), dk], writes=[dk])
                    T.op("dve", lambda e, b=b, dst=dst, w0=w0: e.scalar_tensor_tensor(
                        out=dst[:, 2:512], in0=bank(b)[:, 0:510], scalar=w0, in1=dst[:, 2:512], op0=ALU.mult,
                        op1=ALU.add), reads=[bkey(b), dk], writes=[dk])
                    if J > 0:
                        T.op("dve", lambda e, dst=dst, w0=w0, ch=ch: e.scalar_tensor_tensor(
                            out=dst[:, 0:2], in0=halo[:, 1 - par, ch, 0:2], scalar=w0, in1=dst[:, 0:2], op0=ALU.mult,
                            op1=ALU.add), reads=[("halo", 1 - par, ch), dk], writes=[dk])
                        T.op("dve", lambda e, dst=dst, w1=w1, ch=ch: e.scalar_tensor_tensor(
                            out=dst[:, 0:1], in0=halo[:, 1 - par, ch, 1:2], scalar=w1, in1=dst[:, 0:1], op0=ALU.mult,
                            op1=ALU.add), reads=[("halo", 1 - par, ch), dk], writes=[dk])
                    if J < 3:
                        T.op("act", lambda e, b=b, ch=ch: e.activation(out=halo[:, par, ch, 0:2], in_=bank(b)[:, 510:512],
                                                                       func=AF.Copy),
                             reads=[bkey(b)], writes=[("halo", par, ch)])
                T.op("act", lambda e, t=t: e.activation(out=ga[t][:, :], in_=ga[t][:, :], func=AF.Silu),
                     reads=[("ga", t)], writes=[("ga", t)])
                T.op(PENG, lambda e, t=t, g=g: e.tensor_tensor(out=actb[ab][:, g, :], in0=ga[t][:, :], in1=va[t][:, :],
                                                                op=ALU.mult),
                     reads=[("ga", t), ("va", t)], writes=[("actb", ab, g)])
        for j in range(4):
            i = 4 * J + j
            for half in range(2):
                b = next_bank([1, 5, 6, 7])
                for g in range(22):
                    T.op("pe", lambda e, g=g, j=j, b=b, half=half: e.matmul(
                        bank(b), lhsT=actb[ab][:, g, j * 128:(j + 1) * 128], rhs=wd[:, g, half * 512:(half + 1) * 512],
                        start=(g == 0), stop=(g == 21)), reads=[("actb", ab, g), ("wd", 0 if g < 11 else 1)], writes=[bkey(b)])
                T.op("dve", lambda e, i=i, b=b, half=half: e.tensor_tensor(
                    out=x_sb[:, i, half * 512:(half + 1) * 512], in0=bank(b), in1=x_sb[:, i, half * 512:(half + 1) * 512],
                    op=ALU.add), reads=[bkey(b), ("x", i)], writes=[("x", i)])

    rk = [0]
    for l in range(nlayers):
        emit_layer_setup(l)
        load_bc(gbc[:, :], rows_d[l, RW_GMIX:RW_GMIX + D], "gbc", "r3")
        def stage_a(J):
            T.mark("L%d A%d norm" % (l, J))
            emit_norm_to_hT(J)
            for ch_ in ORDER:
                if ch_ in "akvq":
                    emit_win(l, J, ch_)
                elif ch_ == "b":
                    emit_sgu(J)
                elif ch_ == "c":
                    emit_conv(J)

        def stage_b(J):
            T.mark("L%d B%d att" % (l, J))
            emit_attention(J)
            T.mark("L%d B%d wout" % (l, J))
            emit_wout(l, J)

        T.rank = rk[0]
        stage_a(0)
        for J in range(4):
            if J + 1 < 4:
                T.rank = rk[0] + (2 * J + 1) * RANKED
                stage_a(J + 1)
            T.rank = rk[0] + (2 * J + 1) * RANKED
            stage_b(J)
        rk[0] += 10
        T.rank = rk[0]
        T.barrier()
        load_bc(gbc[:, :], rows_d[l, RW_GFFN:RW_GFFN + D], "gbc", "r3")
        for hh in range(2):
            src = w_down_d[l].rearrange("(g p) c -> p g c", p=128)[:, 11 * hh:11 * hh + 11, :]
            T.dma("pool", lambda e, src=src, hh=hh: e.dma_start(out=wd[:, 11 * hh:11 * hh + 11, :], in_=src),
                  "wd%d" % hh, writes=[("wd", hh)])
        for J in range(4):
            T.rank = rk[0] + J * RANKED_F
            T.mark("L%d F%d" % (l, J))
            emit_ffn(l, J)
        rk[0] += 10
        T.rank = rk[0]
        T.barrier()
    load_bc(gbc[:, :], gfin_d, "gbc", "r3")
    for J in range(4):
        ss = misc[:, 0:4]
        T.op("dve", lambda e: e.memset(ss, 0.0), writes=["ss"])
        T.op("dve", lambda e: e.memset(misc[:, 12:16], 0.0), writes=["ss"])
        for j in range(4):
            i = 4 * J + j
            for hf in range(2):
                T.op("act", lambda e, i=i, j=j, hf=hf: e.activation(
                    out=junk[:, :], in_=x_sb[:, i, hf * 512:(hf + 1) * 512], func=AF.Square,
                    accum_out=misc[:, 12 * hf + j:12 * hf + j + 1]),
                    reads=[("x", i), "ss"], writes=["junk", ("ssj", j, hf)])
        T.op("dve", lambda e: e.tensor_tensor(out=misc[:, 4:8], in0=misc[:, 0:4], in1=misc[:, 12:16], op=ALU.add),
             reads=["ss"] + [("ssj", j, hf) for j in range(4) for hf in range(2)], writes=["rs"])
        emit_rstd(misc[:, 4:8], 4, 1.0 / D, "rs")
        for j in range(4):
            i = 4 * J + j
            T.op("dve", lambda e, i=i, j=j: e.scalar_tensor_tensor(
                out=x_sb[:, i, :], in0=x_sb[:, i, :], scalar=misc[:, 4 + j:5 + j], in1=gbc[:, :],
                op0=ALU.mult, op1=ALU.mult), reads=[("x", i), "rs", "gbc"], writes=[("x", i)])
        dst = out_d.rearrange("(i p) d -> p i d", p=128)[:, 4 * J:4 * J + 4, :]
        T.dma("sp", lambda e, J=J, dst=dst: e.dma_start(out=dst, in_=x_sb[:, 4 * J:4 * J + 4, :]), "o%d" % J,
              reads=[("x", 4 * J + j) for j in range(4)])
    T.wait_all_dma("sp")

    T.schedule()
    slots = sorted(T.dma_cnt.keys())
    sem_names = ["s_" + e for e in Tracker.ENG] + ["d_" + s for s in slots]
    import contextlib
    with contextlib.ExitStack() as es:
        sems = {}
        for e in Tracker.ENG:
            sems[e] = es.enter_context(nc.semaphore("s_" + e))
        for s in slots:
            sems[("dma", s)] = es.enter_context(nc.semaphore("d_" + s))
        block = es.enter_context(nc.Block())

        def replay(name, eng):
            for waits, fn, kind in T.ops[name]:
                for p, v in waits:
                    eng.wait_ge(sems[p], v)
                if fn is None:
                    continue
                ins = fn(eng)
                if kind[0] == "eng":
                    ins.then_inc(sems[name], 1)
                elif kind[0] == "dma":
                    ins.then_inc(sems[kind], 16)

        @block.tensor
        def _(e):
            replay("pe", e)

        @block.scalar
        def _(e):
            replay("act", e)

        @block.vector
        def _(e):
            replay("dve", e)

        @block.gpsimd
        def _(e):
            replay("pool", e)

        @block.sync
        def _(e):
            replay("sp", e)
    return nc, T


def host_pack(inputs):
    f = lambda a: np.ascontiguousarray(np.asarray(a, dtype=np.float32))
    L = 2
    pvec = np.zeros((L, 128, NPV), np.float32)
    rows = np.zeros((L, NRW), np.float32)
    conv_w = f(inputs["conv_w"])
    for l in range(L):
        for c in range(2):
            sl = slice(128 * c, 128 * (c + 1))
            pvec[l, :, PV_CW + 31 * c:PV_CW + 31 * (c + 1)] = conv_w[l][:, sl].T
            pvec[l, :, PV_CB + c] = f(inputs["conv_b"])[l, sl]
            pvec[l, :, PV_LG + c] = f(inputs["conv_ln_g"])[l, sl]
            pvec[l, :, PV_LB + c] = f(inputs["conv_ln_b"])[l, sl]
            pvec[l, :, PV_GA + c] = f(inputs["g_out"])[l, sl]
        pvec[l, :, PV_SB:PV_SB + 4] = f(inputs["sgu_b"])[l].T
        fw = f(inputs["ffn_conv_w"])[l]
        fb = f(inputs["ffn_conv_b"])[l]
        for c in range(44):
            sl = slice(128 * c, 128 * (c + 1))
            pvec[l, :, PV_FW + 3 * c:PV_FW + 3 * c + 3] = fw[:, sl].T
            pvec[l, :, PV_FB + c] = fb[sl]
        rows[l, RW_GMIX:RW_GMIX + D] = f(inputs["g_mix"])[l]
        rows[l, RW_GFFN:RW_GFFN + D] = f(inputs["g_ffn"])[l]
        rows[l, RW_GOUT:RW_GOUT + 768] = f(inputs["g_out"])[l, 256:]
        rows[l, RW_LNG:RW_LNG + 256] = f(inputs["sgu_ln_g"])[l]
        rows[l, RW_LNB:RW_LNB + 256] = f(inputs["sgu_ln_b"])[l]
    ident = np.eye(128, dtype=np.float32)
    kk, mm = np.meshgrid(np.arange(128), np.arange(128), indexing="ij")
    maskL = np.where(kk >= mm, NEG, 0.0).astype(np.float32)
    cbf = np.concatenate([ident, maskL], axis=1).astype(ml_dtypes.bfloat16)
    tril = (mm <= kk).astype(np.float32)
    cf32 = np.concatenate([np.ones((128, 128), np.float32), tril], axis=1)
    shared = {
        "w_in": f(inputs["w_in"]), "w_out": f(inputs["w_out"]), "w_up": f(inputs["w_up"]),
        "w_down": f(inputs["w_down"]), "sgu_w": f(inputs["sgu_w"]), "pvec": pvec, "rows": rows,
        "g_final": f(inputs["g_final"]), "cbf": cbf, "cf32": np.ascontiguousarray(cf32),
    }
    return shared


_CACHE = {}


def kernel(**inputs):
    x = np.asarray(inputs["x"], dtype=np.float32)
    shared = host_pack(inputs)
    if "nc" not in _CACHE:
        _CACHE["nc"] = build()[0]
    nc = _CACHE["nc"]
    in_maps = []
    for c in range(8):
        m = dict(shared)
        m["x"] = np.ascontiguousarray(x[c])
        in_maps.append(m)
    res = run_bass_kernel_spmd(nc, in_maps, core_ids=list(range(8)))
    out = np.stack([np.asarray(r["out"], dtype=np.float32) for r in res.results], axis=0)
    return out
```
